# Optimizing a Trainium2 kernel written in Bass

```python
import math
import jax, jax.numpy as jnp
from jax import lax
import numpy as np

D_MODEL = 2048
BATCH = 4
SEQ = 2048
DEPTH = 4
DEC_BATCH = 128
DEC_SEQ = 8
PAST_LEN = 16384
PAGE_SIZE = 128

N_META = 16
CHUNK = 64
NORM_EPS = 1e-6

RW_HEADS = 8
RW_HEAD = 64
RW_W = RW_HEADS * RW_HEAD
RW_DECAY_RANK = 64
RW_A_RANK = 64
RW_GATE_RANK = 128
RW_COLS = 3 * RW_W + RW_DECAY_RANK + RW_A_RANK + RW_GATE_RANK
RW_LN_EPS = 64e-5
RW_SPLITS = (RW_W, 2 * RW_W, 3 * RW_W, 3 * RW_W + RW_DECAY_RANK, 3 * RW_W + RW_DECAY_RANK + RW_A_RANK)

RET_HEADS = 4
RET_DK = 64
RET_DV = 128
RET_W = RET_HEADS * RET_DV
RET_COLS = 2 * RET_HEADS * RET_DK + 2 * RET_W
RET_SPLITS = (RET_HEADS * RET_DK, 2 * RET_HEADS * RET_DK, 2 * RET_HEADS * RET_DK + RET_W)
ROPE_BASE = 10000.0

HG_HEADS = 4
HG_DK = 128
HG_DV = 128
HG_W = HG_HEADS * HG_DV
HG_COLS = 2 * HG_HEADS * HG_DK + 2 * HG_W
HG_SPLITS = (HG_HEADS * HG_DK, 2 * HG_HEADS * HG_DK, 2 * HG_HEADS * HG_DK + HG_W)

GD_HEADS = 4
GD_DK = 128
GD_DV = 128
GD_W = GD_HEADS * GD_DV
GD_CONV = 4
GD_CONV_COLS = 2 * GD_HEADS * GD_DK + GD_W
GD_COLS = GD_CONV_COLS + 2 * GD_HEADS + GD_W
GD_SPLITS = (GD_CONV_COLS, GD_CONV_COLS + GD_HEADS, GD_CONV_COLS + 2 * GD_HEADS)
GD_QKV_SPLITS = (GD_HEADS * GD_DK, 2 * GD_HEADS * GD_DK)

N_BRANCH = 4
BR_W = 512
GATE_COLS = N_BRANCH * D_MODEL
IN_COLS = RW_COLS + RET_COLS + HG_COLS + GD_COLS + GATE_COLS
IN_SPLITS = (RW_COLS, RW_COLS + RET_COLS, RW_COLS + RET_COLS + HG_COLS, RW_COLS + RET_COLS + HG_COLS + GD_COLS)
FF = ((8 * D_MODEL // 3 + 255) // 256) * 256

kernel_name = 'hybrid_gated_branch_decoder_step'


def _rmsnorm(x, g):
    xf = x.astype(jnp.float32)
    y = xf * lax.rsqrt(jnp.mean(xf * xf, axis=-1, keepdims=True) + NORM_EPS)
    return (y * g.astype(jnp.float32)).astype(x.dtype)


def _rms_heads(x):
    return x * lax.rsqrt(jnp.mean(x * x, axis=-1, keepdims=True) + NORM_EPS)


def _groupnorm(x, g, b):
    mu = jnp.mean(x, axis=-1, keepdims=True)
    var = jnp.mean(jnp.square(x - mu), axis=-1, keepdims=True)
    return (x - mu) * lax.rsqrt(var + RW_LN_EPS) * g + b


def _l2norm(x):
    return x * lax.rsqrt(jnp.sum(x * x, axis=-1, keepdims=True) + 1e-6)


def _rotary(x, pos):
    half = x.shape[-1] // 2
    inv = ROPE_BASE ** (-jnp.arange(half, dtype=jnp.float32) / half)
    ang = pos.astype(jnp.float32)[:, None] * inv[None, :]
    cos = jnp.cos(ang)[None, :, None, :]
    sin = jnp.sin(ang)[None, :, None, :]
    x1, x2 = x[..., :half], x[..., half:]
    return jnp.concatenate([x1 * cos - x2 * sin, x1 * sin + x2 * cos], axis=-1)


def _masked_exp(diff, mask):
    return jnp.where(mask, jnp.exp(jnp.where(mask, diff, 0.0)), 0.0)


def _rwkv7_scan(S, r, w, k, v, a, b):
    def step(S, inp):
        r_t, w_t, k_t, v_t, a_t, b_t = inp
        sa = jnp.einsum('bhvk,bhk->bhv', S, a_t)
        S = S * w_t[:, :, None, :] + sa[..., None] * b_t[:, :, None, :] + v_t[..., None] * k_t[:, :, None, :]
        return S, jnp.einsum('bhvk,bhk->bhv', S, r_t)
    S, o = lax.scan(step, S, tuple(jnp.moveaxis(z, 1, 0) for z in (r, w, k, v, a, b)))
    return S, jnp.moveaxis(o, 0, 1)


def _decay_chunk(S, q, k, v, g):
    C = q.shape[2]
    b = jnp.cumsum(g, axis=2)
    causal = jnp.tril(jnp.ones((C, C), dtype=bool))
    dmat = _masked_exp(b[..., :, None] - b[..., None, :], causal)
    scores = jnp.einsum('bhtd,bhsd->bhts', q, k) * dmat
    o = jnp.einsum('bhts,bhsv->bhtv', scores, v) + jnp.exp(b)[..., None] * jnp.einsum('bhtd,bhdv->bhtv', q, S)
    b_last = b[..., -1:]
    S_new = jnp.exp(b_last)[..., None] * S + jnp.einsum('bhsd,bhsv->bhdv', k * jnp.exp(b_last - b)[..., None], v)
    return S_new, o


def _gla_chunk(S, q, k, v, g):
    C = q.shape[2]
    b = jnp.cumsum(g, axis=2)
    causal = jnp.tril(jnp.ones((C, C), dtype=bool))[None, None, :, :, None]
    dec = _masked_exp(b[:, :, :, None, :] - b[:, :, None, :, :], causal)
    scores = jnp.einsum('bhtsd,bhsd->bhts', q[:, :, :, None, :] * dec, k)
    o = jnp.einsum('bhts,bhsv->bhtv', scores, v) + jnp.einsum('bhtd,bhdv->bhtv', q * jnp.exp(b), S)
    b_last = b[:, :, -1:]
    S_new = jnp.exp(b_last[:, :, 0])[..., None] * S + jnp.einsum('bhsd,bhsv->bhdv', k * jnp.exp(b_last - b), v)
    return S_new, o


def _delta_chunk(S, q, k, v, beta, g):
    C = q.shape[2]
    dk = q.shape[-1]
    b = jnp.cumsum(g, axis=2)
    causal = jnp.tril(jnp.ones((C, C), dtype=bool))
    strict = jnp.tril(jnp.ones((C, C), dtype=bool), -1)
    dmat = _masked_exp(b[..., :, None] - b[..., None, :], causal)
    A = beta[..., :, None] * jnp.einsum('bhtd,bhsd->bhts', k, k) * jnp.where(strict, dmat, 0.0)
    lhs = jnp.eye(C, dtype=A.dtype) + A
    rhs = jnp.concatenate([(beta * jnp.exp(b))[..., None] * k, beta[..., None] * v], axis=-1)
    sol = lax.linalg.triangular_solve(lhs, rhs, left_side=True, lower=True, unit_diagonal=True)
    delta = sol[..., dk:] - jnp.einsum('bhtd,bhdv->bhtv', sol[..., :dk], S)
    o = jnp.exp(b)[..., None] * jnp.einsum('bhtd,bhdv->bhtv', q, S) + jnp.einsum('bhts,bhsv->bhtv', jnp.einsum('bhtd,bhsd->bhts', q, k) * dmat, delta)
    b_last = b[..., -1:]
    S_new = jnp.exp(b_last)[..., None] * S + jnp.einsum('bhsd,bhsv->bhdv', k * jnp.exp(b_last - b)[..., None], delta)
    return S_new, o


def _chunked_scan(chunk_fn, state, xs, n_lead):
    T = xs[0].shape[2]
    outs = []
    if n_lead > 0:
        state, o = chunk_fn(state, *(z[:, :, :n_lead] for z in xs))
        outs.append(o)
    n_chunks = (T - n_lead) // CHUNK
    if n_chunks > 0:
        blocks = tuple(jnp.moveaxis(z[:, :, n_lead:].reshape(z.shape[:2] + (n_chunks, CHUNK) + z.shape[3:]), 2, 0) for z in xs)
        state, o = lax.scan(lambda s, blk: chunk_fn(s, *blk), state, blocks)
        o = jnp.moveaxis(o, 0, 2)
        outs.append(o.reshape(o.shape[:2] + (n_chunks * CHUNK,) + o.shape[4:]))
    out = jnp.concatenate(outs, axis=2) if len(outs) > 1 else outs[0]
    return state, out


def _layer(x, pos, states, P):
    st_wkv, st_shift, st_ret, st_hg, st_gd, st_conv = states
    f32 = jnp.float32
    bn, t_len, _ = x.shape
    n_lead = t_len % CHUNK
    h = _rmsnorm(x, P['norm_mix'])
    proj = (h @ P['w_in']).astype(f32)
    pa, pb, pc, pd, pg = jnp.split(proj, IN_SPLITS, axis=-1)

    def heads(z, n_h):
        return z.reshape(bn, t_len, n_h, -1)

    def bh(z):
        return jnp.swapaxes(z, 1, 2)

    prev = jnp.concatenate([st_shift.astype(f32)[:, None], pa[:, :-1]], axis=1)
    xm = pa + (prev - pa) * P['rw_mu']
    r, k, v, wl, al, gl = jnp.split(xm, RW_SPLITS, axis=-1)
    w_log = -jax.nn.softplus(-(P['rw_w0'] + jnp.tanh(wl) @ P['rw_w2'])) - 0.5
    decay = jnp.exp(-jnp.exp(w_log))
    a = jax.nn.sigmoid(P['rw_a0'] + al @ P['rw_a2'])
    g_a = jax.nn.sigmoid(gl) @ P['rw_g2']
    kk = _l2norm(heads(k * P['rw_kk'], RW_HEADS))
    k = k * (1.0 + (a - 1.0) * P['rw_ka'])
    r, k, v, decay, a = (heads(z, RW_HEADS) for z in (r, k, v, decay, a))
    wkv, oa = _rwkv7_scan(st_wkv.astype(f32), r, decay, k, v, -kk, kk * a)
    oa = _groupnorm(oa, P['rw_ln_g'].reshape(RW_HEADS, RW_HEAD), P['rw_ln_b'].reshape(RW_HEADS, RW_HEAD))
    oa = oa + jnp.sum(r * k * P['rw_rk'], axis=-1, keepdims=True) * v
    oa = oa.reshape(bn, t_len, RW_W) * g_a

    q, k, v, g_b = jnp.split(pb, RET_SPLITS, axis=-1)
    q = _rotary(heads(q, RET_HEADS), pos)
    k = _rotary(heads(k, RET_HEADS), pos) * (RET_DK ** -0.5)
    log_gamma = jnp.log1p(-jnp.exp2(-5.0 - jnp.arange(RET_HEADS, dtype=f32)))
    g = jnp.broadcast_to(log_gamma[None, :, None], (bn, RET_HEADS, t_len))
    ret, ob = _chunked_scan(_decay_chunk, st_ret.astype(f32), (bh(q), bh(k), bh(heads(v, RET_HEADS)), g), n_lead)
    ob = _rms_heads(bh(ob)).reshape(bn, t_len, RET_W) * jax.nn.silu(g_b)

    q, f, i, g_c = jnp.split(pc, HG_SPLITS, axis=-1)
    lb = P['hg_lb']
    log_f = jnp.log(lb + (1.0 - lb) * jax.nn.sigmoid(f))
    k_in = (1.0 - lb) * jax.nn.sigmoid(-f)
    hg, oc = _chunked_scan(_gla_chunk, st_hg.astype(f32),
                           (bh(heads(jax.nn.silu(q), HG_HEADS)), bh(heads(k_in, HG_HEADS)),
                            bh(heads(i, HG_HEADS)), bh(heads(log_f, HG_HEADS))), n_lead)
    oc = (_rms_heads(bh(oc)) * P['hg_norm_g'].reshape(HG_HEADS, HG_DV)).reshape(bn, t_len, HG_W) * jax.nn.silu(g_c)

    qkv_in, beta_in, alpha_in, g_d = jnp.split(pd, GD_SPLITS, axis=-1)
    full = jnp.concatenate([st_conv.astype(f32), qkv_in], axis=1)
    conv = full[:, :t_len] * P['gd_conv'][0]
    for j in range(1, GD_CONV):
        conv = conv + full[:, j:j + t_len] * P['gd_conv'][j]
    q, k, v = jnp.split(jax.nn.silu(conv), GD_QKV_SPLITS, axis=-1)
    q = _l2norm(heads(q, GD_HEADS)) * (GD_DK ** -0.5)
    k = _l2norm(heads(k, GD_HEADS))
    beta = jax.nn.sigmoid(beta_in)
    g = -jnp.exp(P['gd_a_log']) * jax.nn.softplus(alpha_in + P['gd_dt_bias'])
    gd, od = _chunked_scan(_delta_chunk, st_gd.astype(f32),
                           (bh(q), bh(k), bh(heads(v, GD_HEADS)), bh(beta), bh(g)), n_lead)
    od = (_rms_heads(bh(od)) * P['gd_norm_g'].reshape(GD_HEADS, GD_DV)).reshape(bn, t_len, GD_W) * jax.nn.silu(g_d)
    new_conv = full[:, t_len:]

    gates = jax.nn.sigmoid(pg).reshape(bn, t_len, N_BRANCH, D_MODEL)
    merged = jnp.zeros((bn, t_len, D_MODEL), f32)
    for n, o_n in enumerate((oa, ob, oc, od)):
        merged = merged + gates[:, :, n] * (o_n @ P['w_branch'][n])
    x = x + (merged @ P['w_out']).astype(x.dtype)

    h2 = _rmsnorm(x, P['norm_ffn'])
    up, gate = jnp.split(h2 @ P['w_up'], [FF], axis=-1)
    x = x + ((jax.nn.silu(gate) * up) @ P['w_down']).astype(x.dtype)
    return x, (wkv, pa[:, -1], ret, hg, gd, new_conv)


def setup_inputs(seed: int = 0) -> dict:
    key = jax.random.key(seed)
    ks = jax.random.split(key, 40)
    f32 = jnp.float32

    def nrm(i, shape, scale):
        return scale * jax.random.normal(ks[i], shape, f32)

    def gain(i, shape):
        return 1.0 + 0.05 * jax.random.normal(ks[i], shape, f32)

    dt = jnp.exp(jax.random.uniform(ks[30], (DEPTH, GD_HEADS), f32, math.log(1e-3), math.log(1e-1)))
    return {
        'x_prompt': nrm(0, (BATCH, SEQ, D_MODEL), 1.0),
        'x_sample': nrm(1, (DEC_BATCH, DEC_SEQ, D_MODEL), 1.0),
        'state_rwkv_wkv': nrm(2, (DEPTH, DEC_BATCH, RW_HEADS, RW_HEAD, RW_HEAD), 0.5),
        'state_rwkv_shift': nrm(3, (DEPTH, DEC_BATCH, RW_COLS), 1.0),
        'state_ret': nrm(4, (DEPTH, DEC_BATCH, RET_HEADS, RET_DK, RET_DV), 0.5),
        'state_hgrn': nrm(5, (DEPTH, DEC_BATCH, HG_HEADS, HG_DK, HG_DV), 0.3),
        'state_gdn': nrm(6, (DEPTH, DEC_BATCH, GD_HEADS, GD_DK, GD_DV), 0.3),
        'state_gdn_conv': nrm(7, (DEPTH, DEC_BATCH, GD_CONV - 1, GD_CONV_COLS), 1.0),
        'meta_tokens': nrm(8, (N_META, D_MODEL), 1.0),
        'norm_mix': gain(9, (DEPTH, D_MODEL)),
        'w_in': nrm(10, (DEPTH, D_MODEL, IN_COLS), D_MODEL ** -0.5),
        'rw_mu': jax.random.uniform(ks[11], (DEPTH, RW_COLS), f32),
        'rw_w0': -1.0 + nrm(12, (DEPTH, RW_W), 0.5),
        'rw_w2': nrm(13, (DEPTH, RW_DECAY_RANK, RW_W), 0.5 * RW_DECAY_RANK ** -0.5),
        'rw_a0': nrm(14, (DEPTH, RW_W), 0.1),
        'rw_a2': nrm(15, (DEPTH, RW_A_RANK, RW_W), 0.5 * RW_A_RANK ** -0.5),
        'rw_g2': nrm(16, (DEPTH, RW_GATE_RANK, RW_W), RW_GATE_RANK ** -0.5),
        'rw_kk': 0.85 + nrm(17, (DEPTH, RW_W), 0.05),
        'rw_ka': gain(18, (DEPTH, RW_W)),
        'rw_rk': nrm(19, (DEPTH, RW_HEADS, RW_HEAD), 0.1),
        'rw_ln_g': gain(20, (DEPTH, RW_W)),
        'rw_ln_b': nrm(21, (DEPTH, RW_W), 0.02),
        'hg_lb': nrm(22, (DEPTH, HG_HEADS * HG_DK), 1.0),
        'hg_norm_g': gain(23, (DEPTH, HG_W)),
        'gd_conv': nrm(24, (DEPTH, GD_CONV, GD_CONV_COLS), GD_CONV ** -0.5),
        'gd_a_log': jnp.log(jax.random.uniform(ks[25], (DEPTH, GD_HEADS), f32, 1.0, 16.0)),
        'gd_dt_bias': dt + jnp.log(-jnp.expm1(-dt)),
        'gd_norm_g': gain(26, (DEPTH, GD_W)),
        'w_branch': nrm(27, (DEPTH, N_BRANCH, BR_W, D_MODEL), BR_W ** -0.5),
        'w_out': nrm(28, (DEPTH, D_MODEL, D_MODEL), D_MODEL ** -0.5),
        'norm_ffn': gain(29, (DEPTH, D_MODEL)),
        'w_up': nrm(31, (DEPTH, D_MODEL, 2 * FF), D_MODEL ** -0.5),
        'w_down': nrm(32, (DEPTH, FF, D_MODEL), FF ** -0.5),
        'norm_final': gain(33, (D_MODEL,)),
    }


def reference(x_prompt, x_sample, state_rwkv_wkv, state_rwkv_shift, state_ret, state_hgrn, state_gdn, state_gdn_conv,
              meta_tokens, norm_mix, w_in, rw_mu, rw_w0, rw_w2, rw_a0, rw_a2, rw_g2, rw_kk, rw_ka, rw_rk,
              rw_ln_g, rw_ln_b, hg_lb, hg_norm_g, gd_conv, gd_a_log, gd_dt_bias, gd_norm_g,
              w_branch, w_out, norm_ffn, w_up, w_down, norm_final):
    f32 = jnp.float32
    lb_soft = jax.nn.softmax(hg_lb.astype(f32), axis=0)
    lb_all = jnp.cumsum(lb_soft, axis=0) - lb_soft[0]

    bp = x_prompt.shape[0]
    meta = jnp.broadcast_to(meta_tokens.astype(x_prompt.dtype)[None], (bp, N_META, D_MODEL))
    xp = jnp.concatenate([meta, x_prompt], axis=1)
    xs = x_sample
    pos_p = jnp.arange(xp.shape[1], dtype=jnp.int32)
    pos_s = PAST_LEN + jnp.arange(xs.shape[1], dtype=jnp.int32)

    sample_states = (state_rwkv_wkv, state_rwkv_shift, state_ret, state_hgrn, state_gdn, state_gdn_conv)
    new_p = [[] for _ in sample_states]
    new_s = [[] for _ in sample_states]
    for l in range(DEPTH):
        P = {
            'norm_mix': norm_mix[l], 'w_in': w_in[l],
            'rw_mu': rw_mu[l], 'rw_w0': rw_w0[l], 'rw_w2': rw_w2[l], 'rw_a0': rw_a0[l], 'rw_a2': rw_a2[l],
            'rw_g2': rw_g2[l], 'rw_kk': rw_kk[l], 'rw_ka': rw_ka[l], 'rw_rk': rw_rk[l],
            'rw_ln_g': rw_ln_g[l], 'rw_ln_b': rw_ln_b[l],
            'hg_lb': lb_all[l], 'hg_norm_g': hg_norm_g[l],
            'gd_conv': gd_conv[l], 'gd_a_log': gd_a_log[l], 'gd_dt_bias': gd_dt_bias[l], 'gd_norm_g': gd_norm_g[l],
            'w_branch': w_branch[l], 'w_out': w_out[l],
            'norm_ffn': norm_ffn[l], 'w_up': w_up[l], 'w_down': w_down[l],
        }
        zero_states = tuple(jnp.zeros((bp,) + s.shape[2:], s.dtype) for s in sample_states)
        xp, st_p = _layer(xp, pos_p, zero_states, P)
        xs, st_s = _layer(xs, pos_s, tuple(s[l] for s in sample_states), P)
        for i in range(len(sample_states)):
            new_p[i].append(st_p[i].astype(sample_states[i].dtype))
            new_s[i].append(st_s[i].astype(sample_states[i].dtype))

    y_prompt = _rmsnorm(xp[:, N_META:], norm_final)
    y_sample = _rmsnorm(xs, norm_final)
    p_wkv, p_shift, p_ret, p_hg, p_gd, p_conv = (jnp.stack(z) for z in new_p)
    s_wkv, s_shift, s_ret, s_hg, s_gd, s_conv = (jnp.stack(z) for z in new_s)
    return (y_prompt, y_sample, p_wkv, p_shift, p_ret, p_hg, p_gd, p_conv, s_wkv, s_shift, s_ret, s_hg, s_gd, s_conv)
```

```python
import os
from contextlib import ExitStack
import numpy as np
import concourse.bass as bass
import concourse.mybir as mybir
from concourse.bass_utils import run_bass_kernel_spmd

F32 = mybir.dt.float32
BF16 = mybir.dt.bfloat16
AF = mybir.ActivationFunctionType
ALU = mybir.AluOpType
AX = mybir.AxisListType

D = 2048
DEPTH = 4
NCORE = 8
NS = 16
TS = 8
TP = 2064
NMETA = 16
T = 128 + TP
NT = 18
TPAD = NT * 128
RW_COLS = 1792
RET_COLS = 1536
HG_COLS = 2048
GD_COLS = 2056
IN_COLS = 15624
OFF_RW = 0
OFF_RET = 1792
OFF_HG = 3328
OFF_GD = 5376
OFF_G = 7432
FF = 5632
EPS = 1e-6


class View:
    __slots__ = ("buf", "ap")

    def __init__(self, buf, ap):
        self.buf = buf
        self.ap = ap

    def __getitem__(self, idx):
        return View(self.buf, self.ap[idx])

    def re(self, pat, **kw):
        return View(self.buf, self.ap.rearrange(pat, **kw))


class Buf:
    def __init__(self, t, name, space, a=None):
        self.t = t
        self.name = name
        self.space = space
        self.a = a if a is not None else (t.ap() if space == "dram" else t[:])
        self.writers = {}
        self.readers = {}
        self.dsem = None
        self.dcnt = 0
        self.group = [self]

    def alias(self, name, a):
        b = Buf(self.t, name, self.space, a=a)
        self.group.append(b)
        b.group = self.group
        return b

    def __getitem__(self, idx):
        return View(self, self.a[idx])

    @property
    def v(self):
        return View(self, self.a)


def bufs_of(views):
    out = []
    for v in views:
        if isinstance(v, Buf):
            if v not in out:
                out.append(v)
        elif isinstance(v, View) and v.buf not in out:
            out.append(v.buf)
    return out


class KB:
    def __init__(self, nc, es):
        self.nc = nc
        self.es = es
        self.eng = {"pe": nc.tensor, "dve": nc.vector, "act": nc.scalar,
                    "pool": nc.gpsimd, "sp": nc.sync}
        self.sem = {}
        self.cnt = {}
        self.semobj = {}
        for e in self.eng:
            s = nc.alloc_semaphore(name=f"s_{e}")
            self.sem[e] = s
            self.semobj[f"s_{e}"] = s
            self.cnt[e] = 0
        self.seen = {e: {} for e in self.eng}
        self.ninstr = 0
        self.dd_sem = nc.alloc_semaphore(name="s_dd")
        self.semobj["s_dd"] = self.dd_sem
        self.dd_cnt = 0
        self.rr = 0

    def sb(self, name, shape, dt=F32):
        return Buf(self.es.enter_context(self.nc.sbuf_tensor(name, list(shape), dt)), name, "sb")

    def ps(self, name, shape, dt=F32):
        return Buf(self.es.enter_context(self.nc.psum_tensor(name, list(shape), dt)), name, "ps")

    def dram(self, name, shape, kind="Internal", dt=F32):
        return Buf(self.nc.dram_tensor(name, list(shape), dt, kind=kind), name, "dram")

    def _wait(self, e, deps):
        seen = self.seen[e]
        for sname, val in deps.items():
            if seen.get(sname, 0) >= val:
                continue
            self.eng[e].wait_ge(self.semobj[sname], val)
            seen[sname] = val

    @staticmethod
    def _deps(reads, writes, acc):
        deps = {}
        for b0 in reads:
            for b in b0.group:
                for s, v in b.writers.items():
                    if deps.get(s, 0) < v:
                        deps[s] = v
        for b0 in writes:
            for b in b0.group:
                for s, v in b.readers.items():
                    if deps.get(s, 0) < v:
                        deps[s] = v
                if not acc:
                    for s, v in b.writers.items():
                        if deps.get(s, 0) < v:
                            deps[s] = v
        return deps

    @staticmethod
    def _commit(ev, reads, writes, acc):
        s, v = ev
        for b in reads:
            if b.readers.get(s, 0) < v:
                b.readers[s] = v
        for b in writes:
            if acc:
                if b.writers.get(s, 0) < v:
                    b.writers[s] = v
            else:
                b.writers = {s: v}
                b.readers = {}

    def op(self, e, fn, reads=(), writes=(), acc=False):
        reads = bufs_of(reads)
        writes = bufs_of(writes)
        self._wait(e, self._deps(reads, writes, acc))
        ins = fn()
        self.cnt[e] += 1
        ins.then_inc(self.sem[e], 1)
        self._commit((f"s_{e}", self.cnt[e]), reads, writes, acc)
        self.ninstr += 1
        return ins

    def dma(self, out, in_, q=None, acc=False):
        if q is None:
            q = "sp"
        reads = [in_.buf]
        writes = [out.buf]
        self._wait(q, self._deps(reads, writes, acc))
        owner = out.buf if out.buf.space == "sb" else (in_.buf if in_.buf.space == "sb" else None)
        if owner is None:
            self.dd_cnt += 16
            sem, sname, val = self.dd_sem, "s_dd", self.dd_cnt
        else:
            if owner.dsem is None:
                owner.dsem = self.nc.alloc_semaphore(name=f"d_{owner.name}")
                self.semobj[f"d_{owner.name}"] = owner.dsem
            owner.dcnt += 16
            sem, sname, val = owner.dsem, f"d_{owner.name}", owner.dcnt
        ins = self.eng[q].dma_start(out=out.ap, in_=in_.ap)
        ins.then_inc(sem, 16)
        self._commit((sname, val), reads, writes, acc)
        self.ninstr += 1
        return ins

    def finish(self, bufs):
        deps = {}
        for b in bufs:
            for s, v in b.writers.items():
                if deps.get(s, 0) < v:
                    deps[s] = v
        self._wait("sp", deps)

    def alt(self):
        self.rr += 1
        return "dve" if self.rr % 2 else "act"

    def tt(self, out, a, b, op, e="dve"):
        return self.op(e, lambda: self.eng[e].tensor_tensor(out=out.ap, in0=a.ap, in1=b.ap, op=op),
                       reads=[a, b], writes=[out])

    def ts(self, out, a, s1, op0, s2=None, op1=None, e="dve"):
        r = [a] + [s for s in (s1, s2) if isinstance(s, View)]
        g = lambda s: s.ap if isinstance(s, View) else s
        if op1 is None:
            return self.op(e, lambda: self.eng[e].tensor_scalar(out=out.ap, in0=a.ap, scalar1=g(s1), scalar2=None, op0=op0),
                           reads=r, writes=[out])
        return self.op(e, lambda: self.eng[e].tensor_scalar(out=out.ap, in0=a.ap, scalar1=g(s1), scalar2=g(s2), op0=op0, op1=op1),
                       reads=r, writes=[out])

    def stt(self, out, a, s, b, op0, op1):
        r = [a, b] + ([s] if isinstance(s, View) else [])
        sv = s.ap if isinstance(s, View) else s
        return self.op("dve", lambda: self.nc.vector.scalar_tensor_tensor(out=out.ap, in0=a.ap, scalar=sv, in1=b.ap, op0=op0, op1=op1),
                       reads=r, writes=[out])

    def act(self, out, a, func, scale=1.0, bias=None):
        r = [a] + ([bias] if isinstance(bias, View) else [])
        kw = {}
        if bias is not None:
            kw["bias"] = bias.ap if isinstance(bias, View) else bias
        sc = scale.ap if isinstance(scale, View) else scale
        if isinstance(scale, View):
            r.append(scale)
        return self.op("act", lambda: self.nc.scalar.activation(out=out.ap, in_=a.ap, func=func, scale=sc, **kw),
                       reads=r, writes=[out])

    def copy(self, out, a, e=None):
        if e is None:
            e = self.alt()
        if e == "act":
            return self.op("act", lambda: self.nc.scalar.copy(out=out.ap, in_=a.ap), reads=[a], writes=[out])
        return self.op(e, lambda: self.eng[e].tensor_copy(out=out.ap, in_=a.ap), reads=[a], writes=[out])

    def red(self, out, a, op=ALU.add, axis=AX.X):
        return self.op("dve", lambda: self.nc.vector.tensor_reduce(out=out.ap, in_=a.ap, op=op, axis=axis),
                       reads=[a], writes=[out])

    def recip(self, out, a):
        return self.op("dve", lambda: self.nc.vector.reciprocal(out=out.ap, in_=a.ap), reads=[a], writes=[out])

    def memset(self, out, val, e="pool"):
        return self.op(e, lambda: self.eng[e].memset(out.ap, val), writes=[out])

    def mm(self, out, lhsT, rhs, start=True, stop=True):
        return self.op("pe", lambda: self.nc.tensor.matmul(out.ap, lhsT=lhsT.ap, rhs=rhs.ap, start=start, stop=stop),
                       reads=[lhsT, rhs], writes=[out], acc=not start)

    def tr(self, out, a, ident):
        n = a.ap.shape[0]
        return self.op("pe", lambda: self.nc.tensor.transpose(out.ap, a.ap, ident.ap[:n, :n]),
                       reads=[a, ident], writes=[out], acc=True)


def bcast_rows(view, nrows):
    ap = view.ap
    if len(ap.shape) == 1:
        ap = ap.rearrange("(o n) -> o n", o=1)
    return View(view.buf, ap.to_broadcast([nrows, ap.shape[-1]]))


class Prog:
    def __init__(self, nlayers=DEPTH, dbg=False, mixtest=None):
        self.nlayers = nlayers
        self.dbg = dbg
        self.mixtest = mixtest
        self.es = ExitStack()
        nc = self.nc = bass.Bass("TRN2", target_bir_lowering=False)
        K = self.K = KB(nc, self.es)
        inp = lambda n, s: K.dram(n, s, "ExternalInput")
        outp = lambda n, s: K.dram(n, s, "ExternalOutput")
        self.x_p = inp("x_p", [2048, D])
        self.x_s = inp("x_s", [128, D])
        self.meta = inp("meta_tokens", [NMETA, D])
        self.st_wkv = inp("st_wkv", [DEPTH, NS, 8, 64, 64])
        self.st_shift = inp("st_shift", [DEPTH, NS, RW_COLS])
        self.st_ret = inp("st_ret", [DEPTH, NS, 4, 64, 128])
        self.st_hg = inp("st_hg", [DEPTH, NS, 4, 128, 128])
        self.st_gd = inp("st_gd", [DEPTH, NS, 4, 128, 128])
        self.st_conv = inp("st_conv", [DEPTH, NS, 3, 1536])
        self.W = {}
        for n, s in [("norm_mix", [DEPTH, D]), ("w_in", [DEPTH, D, IN_COLS]), ("rw_mu", [DEPTH, RW_COLS]),
                     ("rw_w0", [DEPTH, 512]), ("rw_w2", [DEPTH, 64, 512]), ("rw_a0", [DEPTH, 512]),
                     ("rw_a2", [DEPTH, 64, 512]), ("rw_g2", [DEPTH, 128, 512]), ("rw_kk", [DEPTH, 512]),
                     ("rw_ka", [DEPTH, 512]), ("rw_rk", [DEPTH, 512]), ("rw_ln_g", [DEPTH, 512]),
                     ("rw_ln_b", [DEPTH, 512]), ("hg_lb", [DEPTH, 512]), ("hg_norm_g", [DEPTH, 512]),
                     ("gd_conv", [DEPTH, 4, 1536]), ("gd_a_log", [DEPTH, 4]), ("gd_dt_bias", [DEPTH, 4]),
                     ("gd_norm_g", [DEPTH, 512]), ("w_branch", [DEPTH, 4 * 512, D]), ("w_out", [DEPTH, D, D]),
                     ("norm_ffn", [DEPTH, D]), ("w_up", [DEPTH, D, 2 * FF]), ("w_down", [DEPTH, FF, D]),
                     ("norm_final", [1, D])]:
            self.W[n] = inp(n, s)
        self.y_p = outp("y_p", [2048, D])
        self.y_s = outp("y_s", [128, D])
        self.o_p = {"wkv": outp("p_wkv", [DEPTH, 8, 64, 64]), "shift": outp("p_shift", [DEPTH, RW_COLS]),
                    "ret": outp("p_ret", [DEPTH, 4, 64, 128]), "hg": outp("p_hg", [DEPTH, 4, 128, 128]),
                    "gd": outp("p_gd", [DEPTH, 4, 128, 128]), "conv": outp("p_conv", [DEPTH, 3, 1536])}
        self.o_s = {"wkv": outp("s_wkv", [DEPTH, NS, 8, 64, 64]), "shift": outp("s_shift", [DEPTH, NS, RW_COLS]),
                    "ret": outp("s_ret", [DEPTH, NS, 4, 64, 128]), "hg": outp("s_hg", [DEPTH, NS, 4, 128, 128]),
                    "gd": outp("s_gd", [DEPTH, NS, 4, 128, 128]), "conv": outp("s_conv", [DEPTH, NS, 3, 1536])}
        self.RESID = K.dram("RESID", [TPAD, D])
        if mixtest:
            self.PROJ = inp("PROJ", [TPAD, IN_COLS])
        else:
            self.PROJ = K.dram("PROJ", [TPAD, IN_COLS]) if not dbg else outp("PROJ", [TPAD, IN_COLS])
        self.OB = K.dram("OB", [TPAD, D]) if not dbg else outp("OB", [TPAD, D])
        self.ACT = K.dram("ACTS", [TPAD, FF])
        if dbg:
            self.XDBG = outp("XDBG", [TPAD, D])
        self.WB = [K.sb(f"WB{i}", [128, 8192]) for i in range(2)]
        self.FT = [K.sb(f"FT{i}", [128, 2048]) for i in range(6)]
        self.xt = [K.sb(f"xt{i}", [128, D]) for i in range(2)]
        self.ht = K.sb("ht", [128, D])
        self.ot = [K.sb(f"ot{i}", [128, 512]) for i in range(3)]
        self.gB = K.sb("gB", [128, D])
        self.ident = K.sb("ident", [128, 128])
        self.sm = [K.sb(f"sm{i}", [128, 8]) for i in range(4)]
        self.PS = [K.ps(f"PS{i}", [128, 512]) for i in range(8)]
        self.pscnt = 0
        self.otcnt = 0
        self.FTB = []
        for i in range(6):
            v = self.FT[i].a.bitcast(BF16)
            for hf in range(2):
                self.FTB.append(self.FT[i].alias(f"FTB{i}_{hf}", v[:, hf * 2048:(hf + 1) * 2048]))
        self.FTW = [self.FT[i].alias(f"FTW{i}", self.FT[i].a.bitcast(BF16)) for i in range(6)]
        self.WBB = []
        for i in range(2):
            v = self.WB[i].a.bitcast(BF16)
            for hf in range(2):
                self.WBB.append(self.WB[i].alias(f"WBB{i}_{hf}", v[:, hf * 8192:(hf + 1) * 8192]))
        self.WBW = [self.WB[i].alias(f"WBW{i}", self.WB[i].a.bitcast(BF16)) for i in range(2)]
        mixer_init(self)

    def rows(self, ti):
        r0 = ti * 128
        return r0, min(128, T - r0)

    def nextps(self, lo=0, hi=2):
        self.pscnt += 1
        return self.PS[lo + self.pscnt % (hi - lo)]

    def rmsnorm_tile(self, xt, n, gB, out):
        K = self.K
        sm = self.sm[0]
        K.op("act", lambda: self.nc.scalar.activation(out=out.a[:n], in_=xt.a[:n], func=AF.Square,
                                                       accum_out=sm.a[:n, 0:1]),
             reads=[xt], writes=[out, sm])
        K.ts(sm[:n, 1:2], sm[:n, 0:1], 1.0 / D, ALU.mult, EPS, ALU.add)
        K.act(sm[:n, 2:3], sm[:n, 1:2], AF.Sqrt)
        K.recip(sm[:n, 3:4], sm[:n, 2:3])
        K.stt(out[:n], xt[:n], sm[:n, 3:4], gB[:n], ALU.mult, ALU.mult)

    def to_featT(self, src, n, ft, nk):
        K = self.K
        for j in range(0, nk, 4):
            ps = self.nextps(2, 4)
            m = min(4, nk - j)
            for q in range(m):
                K.tr(ps[:, q * 128:q * 128 + n], src[:n, (j + q) * 128:(j + q + 1) * 128], self.ident.v)
            K.copy(ft[:, j * 128:(j + m) * 128].re("p (k t) -> p k t", t=128)[:, :, :n],
                   ps[:, :m * 128].re("p (k t) -> p k t", t=128)[:, :, :n])

    def wload(self, slot, wview, nk, c0, cw):
        self.K.dma(slot[:, :nk * cw].re("p (k c) -> p k c", c=cw),
                   wview.re("(k p) c -> p k c", p=128)[:, :, c0:c0 + cw], q="pool")

    def dense(self, nk, ncols, blk, wview, feat_loader, epilogue, G):
        K = self.K
        nb = (ncols + blk - 1) // blk
        blocks = [(b * blk, min(blk, ncols - b * blk)) for b in range(nb)]
        wcnt = 0
        for g0 in range(0, NT, G):
            tiles = list(range(g0, min(g0 + G, NT)))
            for i, ti in enumerate(tiles):
                feat_loader(ti, self.FTB[i])
            self.wload(self.WBB[wcnt % 4], wview, nk, *blocks[0])
            for b, (c0, cw) in enumerate(blocks):
                wb = self.WBB[wcnt % 4]
                wcnt += 1
                if b + 1 < nb:
                    self.wload(self.WBB[wcnt % 4], wview, nk, *blocks[b + 1])
                for i, ti in enumerate(tiles):
                    r0, n = self.rows(ti)
                    ps = self.nextps(0, 2)
                    for k in range(nk):
                        K.mm(ps[:n, :cw], self.FTB[i][:, k * 128:k * 128 + n], wb[:, k * cw:(k + 1) * cw],
                             start=(k == 0), stop=(k == nk - 1))
                    epilogue(ti, b, c0, cw, ps)

    def next_ot(self):
        self.otcnt += 1
        return self.ot[self.otcnt % 3]

    def setup(self):
        K = self.K
        nc = self.nc
        K.memset(self.ident.v, 1.0)
        K.op("pool", lambda: nc.gpsimd.affine_select(out=self.ident.a, in_=self.ident.a, pattern=[[-1, 128]],
                                                      compare_op=ALU.is_equal, fill=0.0, base=0, channel_multiplier=1),
             reads=[self.ident.v], writes=[self.ident.v])
        K.dma(self.RESID[0:128, :], self.x_s.v, q="sp")
        K.dma(self.RESID[128:128 + NMETA, :], self.meta.v, q="sp", acc=True)
        K.dma(self.RESID[128 + NMETA:T, :], self.x_p.v, q="sp", acc=True)

    def phase_inproj(self, l):
        K = self.K
        K.dma(self.gB.v, bcast_rows(self.W["norm_mix"][l], 128))

        def loader(ti, ft):
            r0, n = self.rows(ti)
            xt = self.xt[ti % 2]
            K.dma(xt[:n], self.RESID[r0:r0 + n, :])
            self.rmsnorm_tile(xt, n, self.gB, self.ht)
            self.to_featT(self.ht, n, ft, 16)

        def epi(ti, b, c0, cw, ps):
            r0, n = self.rows(ti)
            ot = self.next_ot()
            K.copy(ot[:n, :cw], ps[:n, :cw])
            K.dma(self.PROJ[r0:r0 + n, c0:c0 + cw], ot[:n, :cw], acc=True)

        self.dense(16, IN_COLS, 512, self.W["w_in"][l], loader, epi, G=9)

    def phase_mixers_stub(self, l):
        K = self.K
        z = self.next_ot()
        K.memset(z.v, 0.0)
        for ti in range(NT):
            r0, n = self.rows(ti)
            for j in range(4):
                K.dma(self.OB[r0:r0 + n, j * 512:(j + 1) * 512], z[:n, :], acc=True)

    def phase_merge_out(self, l):
        K = self.K
        wbr = self.W["w_branch"][l]
        mg = self.ht
        first = {}

        def loader(ti, ft):
            r0, n = self.rows(ti)
            xt = self.xt[ti % 2]
            K.dma(xt[:n], self.OB[r0:r0 + n, :])
            self.to_featT(xt, n, ft, 16)

        nk = 4
        GM = 6
        for g0 in range(0, NT, GM):
            tiles = list(range(g0, min(g0 + GM, NT)))
            for i, ti in enumerate(tiles):
                loader(ti, self.FTB[i])
            items = [(cb, br) for cb in range(4) for br in range(4)]
            wv = lambda cb, br: wbr[br * 512:(br + 1) * 512, :]
            wc = 0
            self.wload(self.WBB[0], wv(0, 0), nk, 0, 512)
            for it, (cb, br) in enumerate(items):
                c0 = cb * 512
                wb = self.WBB[wc % 4]
                wc += 1
                if it + 1 < len(items):
                    ncb, nbr = items[it + 1]
                    self.wload(self.WBB[wc % 4], wv(ncb, nbr), nk, ncb * 512, 512)
                for i, ti in enumerate(tiles):
                    r0, n = self.rows(ti)
                    ps = self.nextps(0, 2)
                    for k in range(nk):
                        K.mm(ps[:n, :], self.FTB[i][:, (br * 4 + k) * 128:(br * 4 + k) * 128 + n],
                             wb[:, k * 512:(k + 1) * 512], start=(k == 0), stop=(k == nk - 1))
                    gt = self.next_ot()
                    K.dma(gt[:n, :], self.PROJ[r0:r0 + n, OFF_G + br * D + c0:OFF_G + br * D + c0 + 512])
                    K.act(gt[:n, :], gt[:n, :], AF.Sigmoid)
                    acc = self.mgacc[i]
                    if br == 0:
                        K.tt(acc[:n, :], gt[:n, :], ps[:n, :], ALU.mult)
                    else:
                        K.tt(gt[:n, :], gt[:n, :], ps[:n, :], ALU.mult)
                        K.tt(acc[:n, :], acc[:n, :], gt[:n, :], ALU.add)
                    if br == 3:
                        K.dma(self.ACT[r0:r0 + n, c0:c0 + 512], acc[:n, :], acc=True)

        def loader2(ti, ft):
            r0, n = self.rows(ti)
            xt = self.xt[ti % 2]
            K.dma(xt[:n], self.ACT[r0:r0 + n, 0:D])
            self.to_featT(xt, n, ft, 16)

        def epi2(ti, b, c0, cw, ps):
            r0, n = self.rows(ti)
            ot = self.next_ot()
            K.dma(ot[:n, :cw], self.RESID[r0:r0 + n, c0:c0 + cw])
            K.tt(ot[:n, :cw], ot[:n, :cw], ps[:n, :cw], ALU.add)
            K.dma(self.RESID[r0:r0 + n, c0:c0 + cw], ot[:n, :cw], acc=True)

        self.dense(16, D, 512, self.W["w_out"][l], loader2, epi2, G=9)

    def phase_ffn(self, l):
        K = self.K
        K.dma(self.gB.v, bcast_rows(self.W["norm_ffn"][l], 128))
        wup = self.W["w_up"][l]

        def loader(ti, ft):
            r0, n = self.rows(ti)
            xt = self.xt[ti % 2]
            K.dma(xt[:n], self.RESID[r0:r0 + n, :])
            self.rmsnorm_tile(xt, n, self.gB, self.ht)
            self.to_featT(self.ht, n, ft, 16)

        nk = 16
        GU = 9
        wupv = wup.re("(k p) c -> p k c", p=128)
        nbu = FF // 512
        for g0 in range(0, NT, GU):
            tiles = list(range(g0, min(g0 + GU, NT)))
            for i, ti in enumerate(tiles):
                loader(ti, self.FTB[i])

            def wl(b, par):
                for half in range(2):
                    c0 = half * FF + b * 512
                    K.dma(self.WBB[par * 2 + half][:, :nk * 512].re("p (k c) -> p k c", c=512),
                          wupv[:, :, c0:c0 + 512], q="pool")
            wl(0, 0)
            for b in range(nbu):
                c0 = b * 512
                par = b % 2
                if b + 1 < nbu:
                    wl(b + 1, 1 - par)
                wu, wg = self.WBB[par * 2], self.WBB[par * 2 + 1]
                for i, ti in enumerate(tiles):
                    r0, n = self.rows(ti)
                    pu, pg = self.PS[(i % 2) * 2], self.PS[(i % 2) * 2 + 1]
                    for k in range(nk):
                        K.mm(pu[:n, :], self.FTB[i][:, k * 128:k * 128 + n], wu[:, k * 512:(k + 1) * 512],
                             start=(k == 0), stop=(k == nk - 1))
                    for k in range(nk):
                        K.mm(pg[:n, :], self.FTB[i][:, k * 128:k * 128 + n], wg[:, k * 512:(k + 1) * 512],
                             start=(k == 0), stop=(k == nk - 1))
                    ot = self.next_ot()
                    K.act(ot[:n, :], pg[:n, :], AF.Silu)
                    K.tt(ot[:n, :], ot[:n, :], pu[:n, :], ALU.mult)
                    K.dma(self.ACT[r0:r0 + n, c0:c0 + 512], ot[:n, :], acc=True)

        wdn = self.W["w_down"][l]
        nk2 = FF // 128
        GD_ = 3
        blkd = 256
        nbd = D // blkd
        for g0 in range(0, NT, GD_):
            tiles = list(range(g0, min(g0 + GD_, NT)))
            for i, ti in enumerate(tiles):
                r0, n = self.rows(ti)
                for part in range(3):
                    kk0 = part * 16
                    m = min(16, nk2 - kk0)
                    xt = self.xt[part % 2]
                    K.dma(xt[:n, :m * 128], self.ACT[r0:r0 + n, kk0 * 128:(kk0 + m) * 128])
                    ftw = self.FTW[i * 2 + (part // 2)]
                    self.to_featT(xt, n, ftw[:, (part % 2) * 2048:(part % 2) * 2048 + m * 128], m)
            self.wload(self.WBW[0], wdn, nk2, 0, blkd)
            for b in range(nbd):
                c0 = b * blkd
                wb = self.WBW[b % 2]
                if b + 1 < nbd:
                    self.wload(self.WBW[(b + 1) % 2], wdn, nk2, (b + 1) * blkd, blkd)
                for i, ti in enumerate(tiles):
                    r0, n = self.rows(ti)
                    ps = self.nextps(0, 2)
                    for k in range(nk2):
                        ftw = self.FTW[i * 2 + (k // 32)]
                        kk = k % 32
                        K.mm(ps[:n, :blkd], ftw[:, kk * 128:kk * 128 + n], wb[:, k * blkd:(k + 1) * blkd],
                             start=(k == 0), stop=(k == nk2 - 1))
                    ot = self.next_ot()
                    K.dma(ot[:n, :blkd], self.RESID[r0:r0 + n, c0:c0 + blkd])
                    K.tt(ot[:n, :blkd], ot[:n, :blkd], ps[:n, :blkd], ALU.add)
                    K.dma(self.RESID[r0:r0 + n, c0:c0 + blkd], ot[:n, :blkd], acc=True)

    def phase_final(self):
        K = self.K
        K.dma(self.gB.v, bcast_rows(self.W["norm_final"][0], 128))
        for ti in range(NT):
            r0, n = self.rows(ti)
            xt = self.xt[ti % 2]
            K.dma(xt[:n], self.RESID[r0:r0 + n, :])
            if self.dbg:
                K.dma(self.XDBG[r0:r0 + n, :], xt[:n], acc=True)
            self.rmsnorm_tile(xt, n, self.gB, self.ht)
            if ti == 0:
                K.dma(self.y_s.v, self.ht.v, acc=True)
            elif ti == 1:
                K.dma(self.y_p[0:128 - NMETA, :], self.ht[NMETA:128, :], acc=True)
            else:
                p0 = (ti - 1) * 128 - NMETA
                K.dma(self.y_p[p0:p0 + n, :], self.ht[:n, :], acc=True)

    def build(self):
        K = self.K
        self.mgacc = self.mx[0:6]
        self.setup()
        mixer_setup(self)
        if self.mixtest:
            for m in self.mixtest:
                MIXERS[m](self, 0)
            outs = [self.OB] + list(self.o_p.values()) + list(self.o_s.values())
            K.finish(outs)
            return self.nc
        for l in range(self.nlayers):
            self.phase_inproj(l)
            self.phase_mixers(l)
            self.phase_merge_out(l)
            self.phase_ffn(l)
        self.phase_final()
        outs = [self.y_p, self.y_s] + list(self.o_p.values()) + list(self.o_s.values())
        if self.dbg:
            outs += [self.PROJ, self.OB, self.XDBG]
        K.finish(outs)
        return self.nc

    def phase_mixers(self, l):
        mix_rw(self, l)
        mix_ret(self, l)
        mix_hg(self, l)
        mix_gd(self, l)


WNAMES = ["norm_mix", "w_in", "rw_mu", "rw_w0", "rw_w2", "rw_a0", "rw_a2", "rw_g2", "rw_kk", "rw_ka", "rw_rk",
          "rw_ln_g", "rw_ln_b", "hg_lb", "hg_norm_g", "gd_conv", "gd_a_log", "gd_dt_bias", "gd_norm_g",
          "w_branch", "w_out", "norm_ffn", "w_up", "w_down", "norm_final"]


def make_in_maps(inputs, ncores=NCORE):
    f = lambda a: np.ascontiguousarray(np.asarray(a, dtype=np.float32))
    shared = {}
    for n in WNAMES:
        a = f(inputs[n])
        if n == "w_branch":
            a = a.reshape(DEPTH, 4 * 512, D)
        elif n == "rw_rk":
            a = a.reshape(DEPTH, 512)
        elif n == "norm_final":
            a = a.reshape(1, D)
        shared[n] = a
    shared["meta_tokens"] = f(inputs["meta_tokens"])
    shared["consts"], shared["ebc"], shared["rot"] = make_consts()
    maps = []
    for c in range(ncores):
        m = dict(shared)
        m["x_p"] = f(inputs["x_prompt"][c % 4])
        sl = slice(c * NS, (c + 1) * NS)
        m["x_s"] = f(inputs["x_sample"][sl]).reshape(128, D)
        m["st_wkv"] = f(inputs["state_rwkv_wkv"][:, sl])
        m["st_shift"] = f(inputs["state_rwkv_shift"][:, sl])
        m["st_ret"] = f(inputs["state_ret"][:, sl])
        m["st_hg"] = f(inputs["state_hgrn"][:, sl])
        m["st_gd"] = f(inputs["state_gdn"][:, sl])
        m["st_conv"] = f(inputs["state_gdn_conv"][:, sl])
        maps.append(m)
    return maps


def kernel(**inputs):
    prog = Prog()
    nc = prog.build()
    maps = make_in_maps(inputs)
    res = run_bass_kernel_spmd(nc, maps, core_ids=list(range(NCORE)))
    R = res.results
    y_prompt = np.stack([R[b]["y_p"] for b in range(4)], axis=0)
    y_sample = np.concatenate([R[c]["y_s"].reshape(NS, TS, D) for c in range(NCORE)], axis=0)
    outs = [y_prompt, y_sample]
    for k in ["wkv", "shift", "ret", "hg", "gd", "conv"]:
        outs.append(np.stack([R[b]["p_" + k] for b in range(4)], axis=1))
    for k in ["wkv", "shift", "ret", "hg", "gd", "conv"]:
        outs.append(np.concatenate([R[c]["s_" + k] for c in range(NCORE)], axis=1))
    return tuple(np.ascontiguousarray(o, dtype=np.float32) for o in outs)


NCONST = 2048
C_MU_I, C_MU_S, C_ML_S, C_MS_I, C_MS_S, C_MSL_S, C_ID, C_ET, C_GP, C_GS, C_IND = 0, 128, 256, 384, 512, 640, 768, 896, 912, 920, 928
C_MID = 936
C_ONES, C_NI, C_NLS, C_NSI, C_NSLS = 1024, 1152, 1280, 1408, 1536


def make_consts():
    c = np.zeros((128, NCONST), np.float32)
    s = np.arange(128)[:, None]
    t = np.arange(128)[None, :]
    same = (s // 8) == (t // 8)
    c[:, C_MU_I:C_MU_I + 128] = (s <= t)
    c[:, C_MU_S:C_MU_S + 128] = (s < t)
    c[:, C_ML_S:C_ML_S + 128] = (t < s)
    c[:, C_MS_I:C_MS_I + 128] = (s <= t) & same
    c[:, C_MS_S:C_MS_S + 128] = (s < t) & same
    c[:, C_MSL_S:C_MSL_S + 128] = (t < s) & same
    c[:, C_ID:C_ID + 128] = (s == t)
    c[:, C_ET:C_ET + 16] = (np.arange(128)[:, None] // 8) == np.arange(16)[None, :]
    lg = np.log1p(-np.exp2(-5.0 - np.arange(4, dtype=np.float64)))
    i = np.arange(128, dtype=np.float64)[:, None]
    c[:, C_GP:C_GP + 4] = np.exp((i + 1) * lg[None])
    c[:, C_GP + 4:C_GP + 8] = np.exp(-(i + 1) * lg[None]) / 8.0
    i8 = (np.arange(128) % 8).astype(np.float64)[:, None]
    c[:, C_GS:C_GS + 4] = np.exp((i8 + 1) * lg[None])
    c[:, C_GS + 4:C_GS + 8] = np.exp(-(i8 + 1) * lg[None]) / 8.0
    c[:, C_IND] = (np.arange(128) % 64) <= 31
    c[:, C_IND + 1] = 1.0
    c[:64, C_MID:C_MID + 64] = (s[:64] <= t[:, :64]).astype(np.float32) - (np.arange(64)[:, None] <= 31).astype(np.float32)
    c[:, C_ONES:C_ONES + 128] = 1.0
    NEG = -30000.0
    c[:, C_NI:C_NI + 128] = np.where(s <= t, 0.0, NEG)
    c[:, C_NLS:C_NLS + 128] = np.where(t < s, 0.0, NEG)
    c[:, C_NSI:C_NSI + 128] = np.where((s <= t) & same, 0.0, NEG)
    c[:, C_NSLS:C_NSLS + 128] = np.where((t < s) & same, 0.0, NEG)
    eb = np.zeros((128, 16, 128), np.float32)
    eb[:] = ((np.arange(128)[None, :] // 8) == np.arange(16)[:, None])[None]
    rot = np.zeros((TPAD, 64), np.float32)
    pos = np.zeros(TPAD, np.float32)
    pos[0:128] = 16384 + (np.arange(128) % 8)
    pos[128:T] = np.arange(TP)
    inv = (np.float32(10000.0) ** (-np.arange(32, dtype=np.float32) / np.float32(32))).astype(np.float32)
    ang = (pos[:, None].astype(np.float32) * inv[None, :]).astype(np.float32)
    rot[:, :32] = np.cos(ang.astype(np.float64))
    rot[:, 32:] = np.sin(ang.astype(np.float64))
    return c, eb.reshape(128, 2048), rot


RET_LG = [float(np.log1p(-np.exp2(-5.0 - h))) for h in range(4)]


def mixer_init(self):
    K = self.K
    self.CONST = K.dram("consts", [128, NCONST], "ExternalInput")
    self.EBD = K.dram("ebc", [128, 2048], "ExternalInput")
    self.ROT = K.dram("rot", [TPAD, 64], "ExternalInput")
    self.cst = K.sb("cst", [128, NCONST])
    self.Eb = K.sb("Eb", [128, 2048])
    self.rot_t = K.sb("rot_t", [128, 64])
    self.mx = [K.sb(f"mx{i}", [128, 512]) for i in range(8)]
    self.hd = [K.sb(f"hd{i}", [128, 128]) for i in range(24)]
    self.CONVS = K.dram("CONVS", [NS, 11, 1536])
    self.SHS = K.dram("SHS", [NS, 9, RW_COLS])
    self.Hst = K.sb("Hst", [128, 4 * 128])
    self.sm2 = K.sb("smx2", [128, 32])


def mixer_setup(self):
    K = self.K
    K.dma(self.cst.v, self.CONST.v)
    K.dma(self.Eb.v, self.EBD.v)
    K.copy(self.ident.v, self.cst[:, C_ID:C_ID + 128], e="dve")


def mps(self):
    self.pscnt += 1
    return self.PS[4 + self.pscnt % 4]


def headT(self, dst, src, n, dk):
    K = self.K
    ps = mps(self)
    K.tr(ps[:dk, :n], src, self.ident.v)
    K.copy(dst, ps[:dk, :n])


def rms_heads_gate(self, o, n, nh, dv, gate, gain, outv, r0, ocol):
    K = self.K
    sq = self.mx[7]
    sm = self.sm2
    K.tt(sq[:n, :nh * dv], o[:n, :nh * dv], o[:n, :nh * dv], ALU.mult)
    K.red(sm[:n, 0:nh], sq[:n, :nh * dv].re("p (h d) -> p h d", d=dv))
    K.ts(sm[:n, 8:8 + nh], sm[:n, 0:nh], 1.0 / dv, ALU.mult, EPS, ALU.add)
    K.act(sm[:n, 16:16 + nh], sm[:n, 8:8 + nh], AF.Sqrt)
    K.recip(sm[:n, 24:24 + nh], sm[:n, 16:16 + nh])
    for h in range(nh):
        K.ts(o[:n, h * dv:(h + 1) * dv], o[:n, h * dv:(h + 1) * dv], sm[:n, 24 + h:25 + h], ALU.mult)
    if gain is not None:
        K.tt(o[:n, :nh * dv], o[:n, :nh * dv], gain[:n, :nh * dv], ALU.mult)
    K.act(sq[:n, :512], gate, AF.Silu)
    K.tt(o[:n, :512], o[:n, :512], sq[:n, :512], ALU.mult)
    K.dma(self.OB[r0:r0 + n, ocol:ocol + 512], o[:n, :512], acc=True)


def lin_core(self, n, dk, dv, kT, qT, mask, qT_state, Hs, X, ktok, o_out, upd):
    K = self.K
    ps = mps(self)
    K.mm(ps[:n, :n], kT, qT)
    ST = self.hd[9]
    K.tt(ST[:n, :n], ps[:n, :n], mask, ALU.mult)
    po = mps(self)
    for j in range(len(Hs)):
        K.mm(po[:n, :dv], qT_state[j], Hs[j], start=(j == 0), stop=False)
    K.mm(po[:n, :dv], ST[:n, :n], X, start=False, stop=True)
    K.copy(o_out, po[:n, :dv])
    for j0 in range(0, len(Hs), 4):
        pu = mps(self)
        js = list(range(j0, min(j0 + 4, len(Hs))))
        for q, j in enumerate(js):
            K.mm(pu[:dk, q * dv:(q + 1) * dv], ktok[j], X)
        for q, j in enumerate(js):
            upd(Hs[j], pu[:dk, q * dv:(q + 1) * dv])


def mix_ret(self, l):
    K = self.K
    pa = self.xt[0]
    o_all = self.mx[0]
    Hst = self.Hst
    K.memset(Hst.v, 0.0, e="dve")
    SH = self.FT[0]
    for ti in range(NT):
        r0, n = self.rows(ti)
        smp = (ti == 0)
        K.dma(pa[:n, :RET_COLS], self.PROJ[r0:r0 + n, OFF_RET:OFF_RET + RET_COLS])
        K.dma(self.rot_t[:n, :], self.ROT[r0:r0 + n, :])
        cos = self.rot_t[:n, 0:32]
        sin = self.rot_t[:n, 32:64]
        qk = self.mx[1]
        tmp = self.mx[2]
        for w in range(2):
            src = pa[:n, w * 256:(w + 1) * 256].re("p (h two d) -> p h two d", two=2, d=32)
            dst = qk[:n, w * 256:(w + 1) * 256].re("p (h two d) -> p h two d", two=2, d=32)
            tv = tmp[:n, 0:256].re("p (h two d) -> p h two d", two=2, d=32)
            for h in range(4):
                K.tt(dst[:, h, 0, :], src[:, h, 0, :], cos, ALU.mult)
                K.tt(tv[:, h, 0, :], src[:, h, 1, :], sin, ALU.mult)
                K.tt(dst[:, h, 0, :], dst[:, h, 0, :], tv[:, h, 0, :], ALU.subtract)
                K.tt(dst[:, h, 1, :], src[:, h, 0, :], sin, ALU.mult)
                K.tt(tv[:, h, 1, :], src[:, h, 1, :], cos, ALU.mult)
                K.tt(dst[:, h, 1, :], dst[:, h, 1, :], tv[:, h, 1, :], ALU.add)
        gcol = C_GS if smp else C_GP
        mask = self.cst[:n, (C_MS_I if smp else C_MU_I):(C_MS_I if smp else C_MU_I) + n]
        for h in range(4):
            K.ts(qk[:n, h * 64:(h + 1) * 64], qk[:n, h * 64:(h + 1) * 64], self.cst[:n, gcol + h:gcol + h + 1], ALU.mult)
            K.ts(qk[:n, 256 + h * 64:256 + (h + 1) * 64], qk[:n, 256 + h * 64:256 + (h + 1) * 64],
                 self.cst[:n, gcol + 4 + h:gcol + 5 + h], ALU.mult)
        for h in range(4):
            qT, kT = self.hd[0], self.hd[1]
            headT(self, qT[:64, :n], qk[:n, h * 64:(h + 1) * 64], n, 64)
            headT(self, kT[:64, :n], qk[:n, 256 + h * 64:256 + (h + 1) * 64], n, 64)
            X = pa[:n, 512 + h * 128:512 + (h + 1) * 128]
            ktok = qk[:n, 256 + h * 64:256 + (h + 1) * 64]
            if not smp:
                gC = float(np.exp(n * RET_LG[h]))

                def upd(H, psv, gC=gC):
                    K.tt(H, H, psv, ALU.add)
                    K.ts(H, H, gC, ALU.mult)
                lin_core(self, n, 64, 128, kT[:64, :n], qT[:64, :n], mask, [qT[:64, :n]],
                         [Hst[:64, h * 128:(h + 1) * 128]], X, [ktok], o_all[:n, h * 128:(h + 1) * 128], upd)
            else:
                gC = float(np.exp(TS * RET_LG[h]))
                K.dma(SH[:64, :].re("p (s v) -> p s v", v=128), self.st_ret[l, :, h].re("s k v -> k s v"))
                Qm = self.FT[1]
                Km = self.FT[2]
                K.tt(Qm[:64, :].re("p (j t) -> p j t", t=128), self.Eb[:64, :].re("p (j t) -> p j t", t=128),
                     View(qT, qT.a[:64, :].rearrange("p (o t) -> p o t", o=1).to_broadcast([64, 16, 128])), ALU.mult)
                kb = View(ktok.buf, ktok.ap.rearrange("p (o d) -> p o d", o=1).to_broadcast([128, 16, 64]))
                eb = View(self.cst, self.cst.a[:, C_ET:C_ET + 16].rearrange("p (j o) -> p j o", o=1).to_broadcast([128, 16, 64]))
                K.tt(Km[:, :1024].re("p (j d) -> p j d", d=64), kb, eb, ALU.mult)

                def upd(H, psv, gC=gC):
                    K.tt(H, H, psv, ALU.add)
                    K.ts(H, H, gC, ALU.mult)
                lin_core(self, n, 64, 128, kT[:64, :n], qT[:64, :n], mask,
                         [Qm[:64, j * 128:(j + 1) * 128] for j in range(NS)],
                         [SH[:64, j * 128:(j + 1) * 128] for j in range(NS)], X,
                         [Km[:, j * 64:(j + 1) * 64] for j in range(NS)], o_all[:n, h * 128:(h + 1) * 128], upd)
                K.dma(self.o_s["ret"][l, :, h].re("s k v -> k s v"), SH[:64, :].re("p (s v) -> p s v", v=128), acc=True)
        rms_heads_gate(self, o_all, n, 4, 128, pa[:n, 1024:1536], None, None, r0, 512)
    K.dma(self.o_p["ret"][l].re("h k v -> k h v"), Hst[:64, :].re("p (h v) -> p h v", v=128), acc=True)


def mix_hg(self, l):
    K = self.K
    pa = self.xt[0]
    o_all = self.mx[0]
    Hst = self.Hst
    prm = self.xt[1]
    gn = self.mx[6]
    K.dma(prm[:, :].re("p (j c) -> p j c", c=512), View(self.W["hg_lb"], self.W["hg_lb"].a.rearrange("(o j) c -> o j c", o=1).to_broadcast([128, 4, 512])))
    K.act(prm[:, :], prm[:, :], AF.Exp)
    den = self.mx[5]
    K.tt(den[:, :], prm[:, 0:512], prm[:, 512:1024], ALU.add)
    K.tt(den[:, :], den[:, :], prm[:, 1024:1536], ALU.add)
    K.tt(den[:, :], den[:, :], prm[:, 1536:2048], ALU.add)
    K.recip(den[:, :], den[:, :])
    lb = self.mx[4]
    K.memset(lb.v, 0.0, e="dve")
    for j in range(1, l + 1):
        K.tt(lb[:, :], lb[:, :], prm[:, j * 512:(j + 1) * 512], ALU.add)
    K.tt(lb[:, :], lb[:, :], den[:, :], ALU.mult)
    oml = self.mx[5]
    K.ts(oml[:, :], lb[:, :], -1.0, ALU.mult, 1.0, ALU.add)
    K.dma(gn.v, bcast_rows(self.W["hg_norm_g"][l], 128))
    K.memset(Hst.v, 0.0, e="dve")
    SH = self.FT[0]
    chunks = [(0, 128, True)] + [(128 + 64 * i, 64, False) for i in range(32)] + [(128 + 2048, 16, False)]
    for (r0, n, smp) in chunks:
        K.dma(pa[:n, :HG_COLS], self.PROJ[r0:r0 + n, OFF_HG:OFF_HG + HG_COLS])
        sg, logf, kin, cm = self.mx[1], self.mx[2], self.mx[3], self.mx[7]
        K.act(sg[:n, :], pa[:n, 512:1024], AF.Sigmoid)
        K.tt(sg[:n, :], sg[:n, :], oml[:n, :], ALU.mult)
        K.tt(kin[:n, :], oml[:n, :], sg[:n, :], ALU.subtract)
        K.tt(logf[:n, :], sg[:n, :], lb[:n, :], ALU.add)
        K.act(logf[:n, :], logf[:n, :], AF.Ln)
        if smp:
            cmat = self.cst[:n, C_MS_I:C_MS_I + n]
            mask = cmat
        elif n == 64:
            cmat = self.cst[:n, C_MID:C_MID + n]
            mask = self.cst[:n, C_MU_I:C_MU_I + n]
        else:
            cmat = self.cst[:n, C_MU_I:C_MU_I + n]
            mask = cmat
        ps = mps(self)
        K.mm(ps[:n, :512], cmat, logf[:n, :])
        K.act(cm[:n, :], ps[:n, :512], AF.Exp)
        qs = pa
        K.act(sg[:n, :], pa[:n, 0:512], AF.Silu)
        K.tt(qs[:n, 0:512], sg[:n, :], cm[:n, :], ALU.mult)
        K.recip(cm[:n, :], cm[:n, :])
        K.tt(kin[:n, :], kin[:n, :], cm[:n, :], ALU.mult)
        for h in range(4):
            hc = slice(h * 128, (h + 1) * 128)
            qT, kT = self.hd[0], self.hd[1]
            headT(self, qT[:, :n], qs[:n, hc], n, 128)
            headT(self, kT[:, :n], kin[:n, hc], n, 128)
            X = pa[:n, 1024 + h * 128:1024 + (h + 1) * 128]
            ktok = kin[:n, hc]
            sc = self.hd[2]
            if not smp:
                pm = mps(self)
                icol = C_IND if n == 64 else C_IND + 2
                K.mm(pm[:, 0:1], logf[:n, hc], self.cst[:n, icol:icol + 1])
                K.mm(pm[:, 1:2], logf[:n, hc], self.cst[:n, C_IND + 1:C_IND + 2])
                K.copy(sc[:, 0:2], pm[:, 0:2], e="dve")
                K.tt(sc[:, 2:3], sc[:, 1:2], sc[:, 0:1], ALU.subtract)
                K.act(sc[:, 3:4], sc[:, 0:1], AF.Exp)
                K.act(sc[:, 4:5], sc[:, 2:3], AF.Exp)
                Ht = self.hd[3]
                Hv = Hst[:, hc]
                K.ts(Ht[:, :], Hv, sc[:, 3:4], ALU.mult)

                def upd(H, psv, Hv=Hv, sc=sc):
                    K.tt(Hv, H, psv, ALU.add)
                    K.ts(Hv, Hv, sc[:, 4:5], ALU.mult)
                lin_core(self, n, 128, 128, kT[:, :n], qT[:, :n], mask, [qT[:, :n]], [Ht[:, :]], X, [ktok],
                         o_all[:n, hc], upd)
            else:
                pm = mps(self)
                K.mm(pm[:, 0:16], logf[:n, hc], self.cst[:n, C_ET:C_ET + 16])
                K.act(sc[:, 0:16], pm[:, 0:16], AF.Exp)
                K.dma(SH[:, :].re("p (s v) -> p s v", v=128), self.st_hg[l, :, h].re("s k v -> k s v"))
                Qm, Km = self.FT[1], self.FT[2]
                K.tt(Qm[:, :].re("p (j t) -> p j t", t=128), self.Eb[:, :].re("p (j t) -> p j t", t=128),
                     View(qT, qT.a[:, :].rearrange("p (o t) -> p o t", o=1).to_broadcast([128, 16, 128])), ALU.mult)
                kb = View(ktok.buf, ktok.ap.rearrange("p (o d) -> p o d", o=1).to_broadcast([128, 16, 128]))
                eb = View(self.cst, self.cst.a[:, C_ET:C_ET + 16].rearrange("p (j o) -> p j o", o=1).to_broadcast([128, 16, 128]))
                K.tt(Km[:, :].re("p (j d) -> p j d", d=128), kb, eb, ALU.mult)
                jmap = {}

                def upd(H, psv, sc=sc, jmap=jmap):
                    j = jmap["j"]
                    jmap["j"] += 1
                    K.tt(H, H, psv, ALU.add)
                    K.ts(H, H, sc[:, j:j + 1], ALU.mult)
                jmap["j"] = 0
                lin_core(self, n, 128, 128, kT[:, :n], qT[:, :n], mask,
                         [Qm[:, j * 128:(j + 1) * 128] for j in range(NS)],
                         [SH[:, j * 128:(j + 1) * 128] for j in range(NS)], X,
                         [Km[:, j * 128:(j + 1) * 128] for j in range(NS)], o_all[:n, hc], upd)
                K.dma(self.o_s["hg"][l, :, h].re("s k v -> k s v"), SH[:, :].re("p (s v) -> p s v", v=128), acc=True)
        rms_heads_gate(self, o_all, n, 4, 128, pa[:n, 1536:2048], gn, None, r0, 1024)
    K.dma(self.o_p["hg"][l].re("h k v -> k h v"), Hst[:, :].re("p (h v) -> p h v", v=128), acc=True)


def solve_unit(self, n, nsteps, MT, M, MTb, Mb, Dl, dv):
    K = self.K
    cur = (MT, M)
    oth = (MTb, Mb)
    for j in range(nsteps):
        pu = mps(self)
        K.mm(pu[:n, :dv], cur[0][:n, :n], Dl)
        K.tt(Dl, Dl, pu[:n, :dv], ALU.add)
        if j < nsteps - 1:
            p1 = mps(self)
            K.mm(p1[:n, :n], cur[0][:n, :n], cur[1][:n, :n])
            p2 = mps(self)
            K.mm(p2[:n, :n], cur[1][:n, :n], cur[0][:n, :n])
            K.copy(oth[1][:n, :n], p1[:n, :n], e="act")
            K.copy(oth[0][:n, :n], p2[:n, :n], e="dve")
            cur, oth = oth, cur


def mix_gd(self, l):
    K = self.K
    Hst = self.Hst
    o_all = self.mx[0]
    cw = self.WB[0]
    gn = self.mx[6]
    prm = self.mx[4]
    K.dma(cw[:, :4 * 1536].re("p (j c) -> p j c", c=1536),
          View(self.W["gd_conv"], self.W["gd_conv"].a[l:l + 1].to_broadcast([128, 4, 1536])))
    K.dma(gn.v, bcast_rows(self.W["gd_norm_g"][l], 128))
    K.dma(prm[:, 0:4], bcast_rows(self.W["gd_a_log"][l], 128))
    K.dma(prm[:, 4:8], bcast_rows(self.W["gd_dt_bias"][l], 128))
    K.act(prm[:, 8:12], prm[:, 0:4], AF.Exp)
    K.memset(Hst.v, 0.0, e="dve")
    SH = self.FT[0]
    c0 = OFF_GD
    K.dma(self.CONVS[:, 0:3, :], self.st_conv[l], q="sp")
    K.dma(self.CONVS[:, 3:11, :], self.PROJ[0:128, c0:c0 + 1536].re("(s t) c -> s t c", t=TS), q="sp", acc=True)
    for ti in range(NT):
        r0, n = self.rows(ti)
        smp = (ti == 0)
        x0 = self.FT[3]
        shs = [self.FT[4], self.FT[5], self.WB[1]]
        sm = self.mx[5]
        gate = self.mx[1]
        K.dma(x0[:n, :1536], self.PROJ[r0:r0 + n, c0:c0 + 1536])
        for j in (1, 2, 3):
            sh = shs[j - 1]
            if smp:
                for q in range(NS):
                    K.dma(sh[q * TS:(q + 1) * TS, :1536], self.CONVS[q, 3 - j:11 - j, :], acc=(q > 0))
            else:
                K.dma(sh[:n, :1536], self.PROJ[r0 - j:r0 - j + n, c0:c0 + 1536])
                if ti == 1:
                    K.memset(sh[0:j, :1536], 0.0, e="dve")
        K.dma(sm[:n, 0:8], self.PROJ[r0:r0 + n, c0 + 1536:c0 + 1544])
        K.dma(gate[:n, :], self.PROJ[r0:r0 + n, c0 + 1544:c0 + 2056])
        K.tt(x0[:n, :1536], x0[:n, :1536], cw[:n, 3 * 1536:4 * 1536], ALU.mult)
        for j in (1, 2, 3):
            sh = shs[j - 1]
            K.tt(sh[:n, :1536], sh[:n, :1536], cw[:n, (3 - j) * 1536:(4 - j) * 1536], ALU.mult)
            K.tt(x0[:n, :1536], x0[:n, :1536], sh[:n, :1536], ALU.add)
        K.act(x0[:n, :1536], x0[:n, :1536], AF.Silu)
        sq = self.FT[4]
        K.tt(sq[:n, :1024], x0[:n, :1024], x0[:n, :1024], ALU.mult)
        K.red(sm[:n, 8:16], sq[:n, :1024].re("p (h d) -> p h d", d=128))
        K.ts(sm[:n, 8:16], sm[:n, 8:16], 1e-6, ALU.add)
        K.act(sm[:n, 16:24], sm[:n, 8:16], AF.Sqrt)
        K.recip(sm[:n, 24:32], sm[:n, 16:24])
        K.ts(sm[:n, 24:28], sm[:n, 24:28], float(128 ** -0.5), ALU.mult)
        for h in range(8):
            K.ts(x0[:n, h * 128:(h + 1) * 128], x0[:n, h * 128:(h + 1) * 128], sm[:n, 24 + h:25 + h], ALU.mult)
        K.act(sm[:n, 32:36], sm[:n, 0:4], AF.Sigmoid)
        K.tt(sm[:n, 36:40], sm[:n, 4:8], prm[:n, 4:8], ALU.add)
        K.act(sm[:n, 36:40], sm[:n, 36:40], AF.Exp)
        K.act(sm[:n, 36:40], sm[:n, 36:40], AF.Ln, bias=1.0)
        K.tt(sm[:n, 40:44], sm[:n, 36:40], prm[:n, 8:12], ALU.mult)
        K.ts(sm[:n, 40:44], sm[:n, 40:44], -1.0, ALU.mult)
        mI = C_MS_I if smp else C_MU_I
        mLS = C_MSL_S if smp else C_ML_S
        mUS = C_MS_S if smp else C_MU_S
        nI = C_NSI if smp else C_NI
        nLS = C_NSLS if smp else C_NLS
        pc = mps(self)
        K.mm(pc[:n, 0:4], self.cst[:n, mI:mI + n], sm[:n, 40:44])
        K.mm(pc[:n, 4:8], self.cst[:n, mLS:mLS + n], sm[:n, 40:44])
        K.copy(sm[:n, 44:52], pc[:n, 0:8], e="dve")
        K.ts(sm[:n, 52:56], sm[:n, 44:48], -1.0, ALU.mult)
        K.act(sm[:n, 56:60], sm[:n, 44:48], AF.Exp)
        K.act(sm[:n, 60:64], sm[:n, 48:52], AF.Exp)
        K.tt(sm[:n, 64:68], sm[:n, 56:60], sm[:n, 32:36], ALU.mult)
        K.ts(sm[:n, 64:68], sm[:n, 64:68], -1.0, ALU.mult)
        ecC = self.mx[3]
        pcc = mps(self)
        if not smp:
            K.mm(pcc[:, 0:4], self.cst[:n, C_ONES:C_ONES + 128], sm[:n, 40:44])
            K.act(ecC[:, 0:4], pcc[:, 0:4], AF.Exp)
        else:
            gE = self.mx[2]
            for h in range(4):
                K.ts(gE[:n, h * 16:(h + 1) * 16], self.cst[:n, C_ET:C_ET + 16], sm[:n, 40 + h:41 + h], ALU.mult)
            K.mm(pcc[:, 0:64], self.cst[:n, C_ONES:C_ONES + 128], gE[:n, 0:64])
            K.act(ecC[:, 0:64], pcc[:, 0:64], AF.Exp)
        for h in range(4):
            qv = x0[:n, h * 128:(h + 1) * 128]
            kv = x0[:n, 512 + h * 128:512 + (h + 1) * 128]
            vv = x0[:n, 1024 + h * 128:1024 + (h + 1) * 128]
            kT, kbT, qT, qeT, dg, DT, MT, M, MTb, Mb, kb, qe, ks, Dl, t1, DTs = self.hd[0:16]
            K.ts(kb[:n, :], kv, sm[:n, 32 + h:33 + h], ALU.mult)
            K.ts(qe[:n, :], qv, sm[:n, 56 + h:57 + h], ALU.mult)
            K.ts(ks[:n, :], kv, sm[:n, 60 + h:61 + h], ALU.mult)
            headT(self, kT[:, :n], kv, n, 128)
            headT(self, kbT[:, :n], kb[:n, :], n, 128)
            headT(self, qT[:, :n], qv, n, 128)
            headT(self, qeT[:, :n], qe[:n, :], n, 128)
            K.ts(dg[:n, :n], self.cst[:n, C_ID:C_ID + n], sm[:n, 44 + h:45 + h], ALU.mult)
            pR = mps(self)
            K.mm(pR[:n, :n], self.cst[:n, C_ONES:C_ONES + n], dg[:n, :n])
            K.tt(DT[:n, :n], pR[:n, :n], self.cst[:n, nI:nI + n], ALU.add)
            K.act(DT[:n, :n], DT[:n, :n], AF.Exp, bias=sm[:n, 52 + h:53 + h])
            K.tt(M[:n, :n], pR[:n, :n], self.cst[:n, nLS:nLS + n], ALU.subtract)
            K.act(M[:n, :n], M[:n, :n], AF.Exp, scale=-1.0, bias=sm[:n, 44 + h:45 + h])
            K.tt(DTs[:n, :n], DT[:n, :n], self.cst[:n, mUS:mUS + n], ALU.mult)
            pg = mps(self)
            K.mm(pg[:n, :n], kT[:, :n], kbT[:, :n])
            K.stt(MT[:n, :n], pg[:n, :n], -1.0, DTs[:n, :n], ALU.mult, ALU.mult)
            pg2 = mps(self)
            K.mm(pg2[:n, :n], kbT[:, :n], kT[:, :n])
            K.stt(M[:n, :n], pg2[:n, :n], -1.0, M[:n, :n], ALU.mult, ALU.mult)
            if not smp:
                Hs = [Hst[:, h * 128:(h + 1) * 128]]
                kst = [kT[:, :n]]
                qst = [qeT[:, :n]]
                ktl = [ks[:n, :]]
            else:
                K.dma(SH[:, :].re("p (s v) -> p s v", v=128), self.st_gd[l, :, h].re("s k v -> k s v"))
                KTm, QEm, KSm = self.FT[1], self.FT[2], self.FT[5]
                ebv = self.Eb[:, :].re("p (j t) -> p j t", t=128)
                K.tt(KTm[:, :].re("p (j t) -> p j t", t=128), ebv,
                     View(kT, kT.a[:, :].rearrange("p (o t) -> p o t", o=1).to_broadcast([128, 16, 128])), ALU.mult)
                K.tt(QEm[:, :].re("p (j t) -> p j t", t=128), ebv,
                     View(qeT, qeT.a[:, :].rearrange("p (o t) -> p o t", o=1).to_broadcast([128, 16, 128])), ALU.mult)
                kbv = View(ks, ks.a[:, :].rearrange("p (o d) -> p o d", o=1).to_broadcast([128, 16, 128]))
                etv = View(self.cst, self.cst.a[:, C_ET:C_ET + 16].rearrange("p (j o) -> p j o", o=1).to_broadcast([128, 16, 128]))
                K.tt(KSm[:, :].re("p (j d) -> p j d", d=128), kbv, etv, ALU.mult)
                Hs = [SH[:, j * 128:(j + 1) * 128] for j in range(NS)]
                kst = [KTm[:, j * 128:(j + 1) * 128] for j in range(NS)]
                qst = [QEm[:, j * 128:(j + 1) * 128] for j in range(NS)]
                ktl = [KSm[:, j * 128:(j + 1) * 128] for j in range(NS)]
            pk = mps(self)
            for j in range(len(Hs)):
                K.mm(pk[:n, :128], kst[j], Hs[j], start=(j == 0), stop=(j == len(Hs) - 1))
            K.ts(t1[:n, :], vv, sm[:n, 32 + h:33 + h], ALU.mult)
            K.stt(Dl[:n, :], pk[:n, :128], sm[:n, 64 + h:65 + h], t1[:n, :], ALU.mult, ALU.add)
            nsteps = 3 if smp else (7 if n == 128 else 4)
            solve_unit(self, n, nsteps, MT, M, MTb, Mb, Dl[:n, :], 128)
            jm = {"j": 0}

            def upd(H, psv, jm=jm, h=h, smp=smp):
                col = (h * 16 + jm["j"]) if smp else h
                jm["j"] += 1
                K.stt(H, H, ecC[:, col:col + 1], psv, ALU.mult, ALU.add)
            lin_core(self, n, 128, 128, kT[:, :n], qT[:, :n], DT[:n, :n], qst, Hs, Dl[:n, :], ktl,
                     o_all[:n, h * 128:(h + 1) * 128], upd)
            if smp:
                K.dma(self.o_s["gd"][l, :, h].re("s k v -> k s v"), SH[:, :].re("p (s v) -> p s v", v=128), acc=True)
        rms_heads_gate(self, o_all, n, 4, 128, gate[:n, :], gn, None, r0, 1536)
    K.dma(self.o_p["gd"][l].re("h k v -> k h v"), Hst[:, :].re("p (h v) -> p h v", v=128), acc=True)
    K.dma(self.o_p["conv"][l], self.PROJ[T - 3:T, c0:c0 + 1536], q="sp", acc=True)
    K.dma(self.o_s["conv"][l], self.PROJ[0:128, c0:c0 + 1536].re("(s t) c -> s t c", t=TS)[:, TS - 3:TS, :], q="sp", acc=True)


def mix_rw(self, l):
    K = self.K
    W = self.W
    Hst = self.Hst
    PB = self.WB[0]
    PM = self.WB[1]
    o_all = self.mx[0]
    MU, W0, A0, KKP, KA, RKP, LNG, LNB = 0, 1792, 2304, 2816, 3328, 3840, 4352, 4864
    K.dma(PB[:, MU:MU + 1792], bcast_rows(W["rw_mu"][l], 128))
    for nm, off in [("rw_w0", W0), ("rw_a0", A0), ("rw_kk", KKP), ("rw_ka", KA), ("rw_rk", RKP),
                    ("rw_ln_g", LNG), ("rw_ln_b", LNB)]:
        K.dma(PB[:, off:off + 512], bcast_rows(W[nm][l], 128), acc=True)
    K.dma(PM[:64, 0:512], W["rw_w2"][l])
    K.dma(PM[:64, 512:1024], W["rw_a2"][l], acc=True)
    K.dma(PM[:, 1024:1536], W["rw_g2"][l], acc=True)
    K.memset(Hst.v, 0.0, e="dve")
    K.dma(self.SHS[:, 0:1, :], View(self.st_shift, self.st_shift.a[l].rearrange("s (o c) -> s o c", o=1)), q="sp")
    K.dma(self.SHS[:, 1:9, :], self.PROJ[0:128, 0:RW_COLS].re("(s t) c -> s t c", t=TS), q="sp", acc=True)
    SHb = self.FT[0]
    logw, asig, ga, kk = [self.FT[1][:, i * 512:(i + 1) * 512] for i in range(4)]
    qs, ks, as_, bs = [self.FT[2][:, i * 512:(i + 1) * 512] for i in range(4)]
    kp, ec, tA, tB = [self.FT[3][:, i * 512:(i + 1) * 512] for i in range(4)]
    sm = self.mx[1]
    for ti in range(NT):
        r0, n = self.rows(ti)
        smp = (ti == 0)
        pa, xm = self.xt[0], self.xt[1]
        K.dma(pa[:n, :RW_COLS], self.PROJ[r0:r0 + n, 0:RW_COLS])
        if smp:
            for q in range(NS):
                K.dma(xm[q * TS:(q + 1) * TS, :RW_COLS], self.SHS[q, 0:8, :], acc=(q > 0))
        else:
            K.dma(xm[:n, :RW_COLS], self.PROJ[r0 - 1:r0 - 1 + n, 0:RW_COLS])
            if ti == 1:
                K.memset(xm[0:1, :RW_COLS], 0.0, e="dve")
        K.tt(xm[:n, :RW_COLS], xm[:n, :RW_COLS], pa[:n, :RW_COLS], ALU.subtract)
        K.tt(xm[:n, :RW_COLS], xm[:n, :RW_COLS], PB[:n, MU:MU + RW_COLS], ALU.mult)
        K.tt(xm[:n, :RW_COLS], xm[:n, :RW_COLS], pa[:n, :RW_COLS], ALU.add)
        r_, k_, v_ = xm[:n, 0:512], xm[:n, 512:1024], xm[:n, 1024:1536]
        K.act(pa[:n, 0:64], xm[:n, 1536:1600], AF.Tanh)
        K.act(pa[:n, 128:256], xm[:n, 1664:1792], AF.Sigmoid)
        twT, alT, sgT = self.hd[16], self.hd[17], self.hd[18]
        headT(self, twT[:64, :n], pa[:n, 0:64], n, 64)
        headT(self, alT[:64, :n], xm[:n, 1600:1664], n, 64)
        headT(self, sgT[:, :n], pa[:n, 128:256], n, 128)
        pz = mps(self)
        K.mm(pz[:n, :512], twT[:64, :n], PM[:64, 0:512])
        K.tt(logw[:n], pz[:n, :512], PB[:n, W0:W0 + 512], ALU.add)
        K.act(logw[:n], logw[:n], AF.Sigmoid)
        K.ts(logw[:n], logw[:n], -float(np.exp(-0.5)), ALU.mult)
        pz2 = mps(self)
        K.mm(pz2[:n, :512], alT[:64, :n], PM[:64, 512:1024])
        K.tt(asig[:n], pz2[:n, :512], PB[:n, A0:A0 + 512], ALU.add)
        K.act(asig[:n], asig[:n], AF.Sigmoid)
        pz3 = mps(self)
        K.mm(pz3[:n, :512], sgT[:, :n], PM[:, 1024:1536])
        K.copy(ga[:n], pz3[:n, :512], e="act")
        K.tt(kk[:n], k_, PB[:n, KKP:KKP + 512], ALU.mult)
        K.tt(tA[:n], kk[:n], kk[:n], ALU.mult)
        K.red(sm[:n, 0:8], tA[:n].re("p (h d) -> p h d", d=64))
        K.ts(sm[:n, 0:8], sm[:n, 0:8], 1e-6, ALU.add)
        K.act(sm[:n, 8:16], sm[:n, 0:8], AF.Sqrt)
        K.recip(sm[:n, 16:24], sm[:n, 8:16])
        for h in range(8):
            K.ts(kk[:n, h * 64:(h + 1) * 64], kk[:n, h * 64:(h + 1) * 64], sm[:n, 16 + h:17 + h], ALU.mult)
        K.ts(tA[:n], asig[:n], -1.0, ALU.add)
        K.tt(tA[:n], tA[:n], PB[:n, KA:KA + 512], ALU.mult)
        K.ts(tA[:n], tA[:n], 1.0, ALU.add)
        K.tt(kp[:n], k_, tA[:n], ALU.mult)
        mI = C_MS_I if smp else C_MU_I
        pcs = mps(self)
        K.mm(pcs[:n, :512], self.cst[:n, mI:mI + n], logw[:n])
        K.act(ec[:n], pcs[:n, :512], AF.Exp)
        K.tt(qs[:n], r_, ec[:n], ALU.mult)
        K.tt(tB[:n], pcs[:n, :512], logw[:n], ALU.subtract)
        K.act(tB[:n], tB[:n], AF.Exp)
        K.tt(as_[:n], kk[:n], tB[:n], ALU.mult)
        K.ts(as_[:n], as_[:n], -1.0, ALU.mult)
        K.act(ec[:n], pcs[:n, :512], AF.Exp, scale=-1.0)
        K.tt(ks[:n], kp[:n], ec[:n], ALU.mult)
        K.tt(bs[:n], kk[:n], asig[:n], ALU.mult)
        K.tt(bs[:n], bs[:n], ec[:n], ALU.mult)
        mUS = C_MS_S if smp else C_MU_S
        mLS = C_MSL_S if smp else C_ML_S
        nsteps = 3 if smp else (7 if n == 128 else 4)
        for h in range(8):
            hc = slice(h * 64, (h + 1) * 64)
            aT, bT, kT, qT, MT, M, MTb, Mb, AKT, QBT, QKT, U = self.hd[0:12]
            headT(self, aT[:64, :n], as_[:n, hc], n, 64)
            headT(self, bT[:64, :n], bs[:n, hc], n, 64)
            headT(self, kT[:64, :n], ks[:n, hc], n, 64)
            headT(self, qT[:64, :n], qs[:n, hc], n, 64)
            V = xm[:n, 1024 + h * 64:1024 + (h + 1) * 64]
            sc = self.hd[19]
            pm = mps(self)
            if not smp:
                K.mm(pm[:64, 0:1], logw[:n, hc], self.cst[:n, C_IND + 1:C_IND + 2])
                K.act(sc[:64, 0:1], pm[:64, 0:1], AF.Exp)
                Hs = [Hst[:64, hc]]
                aL, qL, bL, kL = [aT[:64, :n]], [qT[:64, :n]], [bs[:n, hc]], [ks[:n, hc]]
            else:
                K.mm(pm[:64, 0:16], logw[:n, hc], self.cst[:n, C_ET:C_ET + 16])
                K.act(sc[:64, 0:16], pm[:64, 0:16], AF.Exp)
                K.dma(SHb[:64, 0:1024].re("p (s k) -> p s k", k=64), self.st_wkv[l, :, h].re("s v k -> v s k"))
                for half in range(2):
                    pt = mps(self)
                    for q in range(8):
                        j = half * 8 + q
                        K.tr(pt[:64, q * 64:(q + 1) * 64], SHb[:64, j * 64:(j + 1) * 64], self.ident.v)
                    K.copy(SHb[:64, 1024 + half * 512:1024 + (half + 1) * 512], pt[:64, :512])
                Hs = [SHb[:64, 1024 + j * 64:1024 + (j + 1) * 64] for j in range(NS)]
                aTm, qTm = self.FT[4], self.FT[5]
                ebv = self.Eb[:64, :].re("p (j t) -> p j t", t=128)
                K.tt(aTm[:64, :].re("p (j t) -> p j t", t=128), ebv,
                     View(aT, aT.a[:64, :].rearrange("p (o t) -> p o t", o=1).to_broadcast([64, 16, 128])), ALU.mult)
                K.tt(qTm[:64, :].re("p (j t) -> p j t", t=128), ebv,
                     View(qT, qT.a[:64, :].rearrange("p (o t) -> p o t", o=1).to_broadcast([64, 16, 128])), ALU.mult)
                etv = View(self.cst, self.cst.a[:, C_ET:C_ET + 16].rearrange("p (j o) -> p j o", o=1).to_broadcast([128, 16, 64]))
                bsv = bs[:n, hc]
                ksv = ks[:n, hc]
                K.tt(PM[:, 2048:3072].re("p (j d) -> p j d", d=64),
                     View(bsv.buf, bsv.ap.rearrange("p (o d) -> p o d", o=1).to_broadcast([128, 16, 64])), etv, ALU.mult)
                K.tt(PM[:, 3072:4096].re("p (j d) -> p j d", d=64),
                     View(ksv.buf, ksv.ap.rearrange("p (o d) -> p o d", o=1).to_broadcast([128, 16, 64])), etv, ALU.mult)
                aL = [aTm[:64, j * 128:(j + 1) * 128] for j in range(NS)]
                qL = [qTm[:64, j * 128:(j + 1) * 128] for j in range(NS)]
                bL = [PM[:, 2048 + j * 64:2048 + (j + 1) * 64] for j in range(NS)]
                kL = [PM[:, 3072 + j * 64:3072 + (j + 1) * 64] for j in range(NS)]
            p1 = mps(self)
            K.mm(p1[:n, :n], bT[:64, :n], aT[:64, :n])
            K.tt(MT[:n, :n], p1[:n, :n], self.cst[:n, mUS:mUS + n], ALU.mult)
            p2 = mps(self)
            K.mm(p2[:n, :n], aT[:64, :n], bT[:64, :n])
            K.tt(M[:n, :n], p2[:n, :n], self.cst[:n, mLS:mLS + n], ALU.mult)
            p3 = mps(self)
            K.mm(p3[:n, :n], kT[:64, :n], aT[:64, :n])
            K.tt(AKT[:n, :n], p3[:n, :n], self.cst[:n, mUS:mUS + n], ALU.mult)
            p4 = mps(self)
            K.mm(p4[:n, :n], bT[:64, :n], qT[:64, :n])
            K.tt(QBT[:n, :n], p4[:n, :n], self.cst[:n, mI:mI + n], ALU.mult)
            p5 = mps(self)
            K.mm(p5[:n, :n], kT[:64, :n], qT[:64, :n])
            K.tt(QKT[:n, :n], p5[:n, :n], self.cst[:n, mI:mI + n], ALU.mult)
            pr = mps(self)
            for j in range(len(Hs)):
                K.mm(pr[:n, :64], aL[j], Hs[j], start=(j == 0), stop=False)
            K.mm(pr[:n, :64], AKT[:n, :n], V, start=False, stop=True)
            K.copy(U[:n, :64], pr[:n, :64], e="dve")
            solve_unit(self, n, nsteps, MT, M, MTb, Mb, U[:n, :64], 64)
            po = mps(self)
            for j in range(len(Hs)):
                K.mm(po[:n, :64], qL[j], Hs[j], start=(j == 0), stop=False)
            K.mm(po[:n, :64], QBT[:n, :n], U[:n, :64], start=False, stop=False)
            K.mm(po[:n, :64], QKT[:n, :n], V, start=False, stop=True)
            K.copy(o_all[:n, hc], po[:n, :64], e="act")
            for j0 in range(0, len(Hs), 8):
                pu = mps(self)
                js = list(range(j0, min(j0 + 8, len(Hs))))
                for q, j in enumerate(js):
                    K.mm(pu[:64, q * 64:(q + 1) * 64], bL[j], U[:n, :64], start=True, stop=False)
                    K.mm(pu[:64, q * 64:(q + 1) * 64], kL[j], V, start=False, stop=True)
                for q, j in enumerate(js):
                    K.tt(Hs[j], Hs[j], pu[:64, q * 64:(q + 1) * 64], ALU.add)
                    K.ts(Hs[j], Hs[j], sc[:64, j:j + 1], ALU.mult)
            if smp:
                for half in range(2):
                    pt = mps(self)
                    for q in range(8):
                        j = half * 8 + q
                        K.tr(pt[:64, q * 64:(q + 1) * 64], Hs[j], self.ident.v)
                    K.copy(SHb[:64, half * 512:(half + 1) * 512], pt[:64, :512])
                K.dma(self.o_s["wkv"][l, :, h].re("s v k -> v s k"), SHb[:64, 0:1024].re("p (s k) -> p s k", k=64), acc=True)
        K.red(sm[:n, 0:8], o_all[:n, :].re("p (h d) -> p h d", d=64))
        K.tt(tA[:n], o_all[:n, :], o_all[:n, :], ALU.mult)
        K.red(sm[:n, 8:16], tA[:n].re("p (h d) -> p h d", d=64))
        K.ts(sm[:n, 0:8], sm[:n, 0:8], 1.0 / 64, ALU.mult)
        K.tt(sm[:n, 16:24], sm[:n, 0:8], sm[:n, 0:8], ALU.mult)
        K.ts(sm[:n, 8:16], sm[:n, 8:16], 1.0 / 64, ALU.mult)
        K.tt(sm[:n, 8:16], sm[:n, 8:16], sm[:n, 16:24], ALU.subtract)
        K.ts(sm[:n, 8:16], sm[:n, 8:16], 64e-5, ALU.add)
        K.act(sm[:n, 16:24], sm[:n, 8:16], AF.Sqrt)
        K.recip(sm[:n, 24:32], sm[:n, 16:24])
        for h in range(8):
            hc = slice(h * 64, (h + 1) * 64)
            K.ts(o_all[:n, hc], o_all[:n, hc], sm[:n, h:h + 1], ALU.subtract, sm[:n, 24 + h:25 + h], ALU.mult)
        K.tt(o_all[:n, :], o_all[:n, :], PB[:n, LNG:LNG + 512], ALU.mult)
        K.tt(o_all[:n, :], o_all[:n, :], PB[:n, LNB:LNB + 512], ALU.add)
        K.tt(tA[:n], r_, kp[:n], ALU.mult)
        K.tt(tA[:n], tA[:n], PB[:n, RKP:RKP + 512], ALU.mult)
        K.red(sm[:n, 32:40], tA[:n].re("p (h d) -> p h d", d=64))
        for h in range(8):
            hc = slice(h * 64, (h + 1) * 64)
            K.stt(o_all[:n, hc], xm[:n, 1024 + h * 64:1024 + (h + 1) * 64], sm[:n, 32 + h:33 + h], o_all[:n, hc], ALU.mult, ALU.add)
        K.tt(o_all[:n, :], o_all[:n, :], ga[:n], ALU.mult)
        K.dma(self.OB[r0:r0 + n, 0:512], o_all[:n, :512], acc=True)
    for half in range(1):
        pt = mps(self)
        for h in range(8):
            K.tr(pt[:64, h * 64:(h + 1) * 64], Hst[:64, h * 64:(h + 1) * 64], self.ident.v)
        K.copy(SHb[:64, 0:512], pt[:64, :512])
    K.dma(self.o_p["wkv"][l].re("h v k -> v h k"), SHb[:64, 0:512].re("p (h k) -> p h k", k=64), acc=True)
    K.dma(self.o_p["shift"][l:l + 1, :], self.PROJ[T - 1:T, 0:RW_COLS], q="sp", acc=True)
    K.dma(self.o_s["shift"][l], self.PROJ[0:128, 0:RW_COLS].re("(s t) c -> s t c", t=TS)[:, TS - 1, :], q="sp", acc=True)


MIXERS = {"ret": mix_ret, "hg": mix_hg, "gd": mix_gd, "rw": mix_rw}
```

```python
import os
from contextlib import ExitStack
import numpy as np
import concourse.bass as bass
import concourse.mybir as mybir
from concourse.bass_utils import run_bass_kernel_spmd

F32 = mybir.dt.float32
BF16 = mybir.dt.bfloat16
AF = mybir.ActivationFunctionType
ALU = mybir.AluOpType
AX = mybir.AxisListType

D = 2048
DEPTH = 4
NCORE = 8
NS = 16
TS = 8
TP = 2064
NMETA = 16
T = 128 + TP
NT = 18
TPAD = NT * 128
RW_COLS = 1792
RET_COLS = 1536
HG_COLS = 2048
GD_COLS = 2056
IN_COLS = 15624
OFF_RW = 0
OFF_RET = 1792
OFF_HG = 3328
OFF_GD = 5376
OFF_G = 7432
FF = 5632
EPS = 1e-6


class View:
    __slots__ = ("buf", "ap")

    def __init__(self, buf, ap):
        self.buf = buf
        self.ap = ap

    def __getitem__(self, idx):
        return View(self.buf, self.ap[idx])

    def re(self, pat, **kw):
        return View(self.buf, self.ap.rearrange(pat, **kw))


class Buf:
    def __init__(self, t, name, space, a=None):
        self.t = t
        self.name = name
        self.space = space
        self.a = a if a is not None else (t.ap() if space == "dram" else t[:])
        self.writers = {}
        self.readers = {}
        self.dsem = None
        self.dcnt = 0
        self.group = [self]

    def alias(self, name, a):
        b = Buf(self.t, name, self.space, a=a)
        self.group.append(b)
        b.group = self.group
        return b

    def __getitem__(self, idx):
        return View(self, self.a[idx])

    @property
    def v(self):
        return View(self, self.a)


def bufs_of(views):
    out = []
    for v in views:
        if isinstance(v, Buf):
            if v not in out:
                out.append(v)
        elif isinstance(v, View) and v.buf not in out:
            out.append(v.buf)
    return out


class KB:
    def __init__(self, nc, es):
        self.nc = nc
        self.es = es
        self.eng = {"pe": nc.tensor, "dve": nc.vector, "act": nc.scalar,
                    "pool": nc.gpsimd, "sp": nc.sync}
        self.sem = {}
        self.cnt = {}
        self.semobj = {}
        for e in self.eng:
            s = nc.alloc_semaphore(name=f"s_{e}")
            self.sem[e] = s
            self.semobj[f"s_{e}"] = s
            self.cnt[e] = 0
        self.seen = {e: {} for e in self.eng}
        self.ninstr = 0
        self.dd_sem = nc.alloc_semaphore(name="s_dd")
        self.semobj["s_dd"] = self.dd_sem
        self.dd_cnt = 0
        self.rr = 0

    def sb(self, name, shape, dt=F32):
        return Buf(self.es.enter_context(self.nc.sbuf_tensor(name, list(shape), dt)), name, "sb")

    def ps(self, name, shape, dt=F32):
        return Buf(self.es.enter_context(self.nc.psum_tensor(name, list(shape), dt)), name, "ps")

    def dram(self, name, shape, kind="Internal", dt=F32):
        return Buf(self.nc.dram_tensor(name, list(shape), dt, kind=kind), name, "dram")

    def _wait(self, e, deps):
        seen = self.seen[e]
        for sname, val in deps.items():
            if seen.get(sname, 0) >= val:
                continue
            self.eng[e].wait_ge(self.semobj[sname], val)
            seen[sname] = val

    @staticmethod
    def _deps(reads, writes, acc):
        deps = {}
        for b0 in reads:
            for b in b0.group:
                for s, v in b.writers.items():
                    if deps.get(s, 0) < v:
                        deps[s] = v
        for b0 in writes:
            for b in b0.group:
                for s, v in b.readers.items():
                    if deps.get(s, 0) < v:
                        deps[s] = v
                if not acc:
                    for s, v in b.writers.items():
                        if deps.get(s, 0) < v:
                            deps[s] = v
        return deps

    @staticmethod
    def _commit(ev, reads, writes, acc):
        s, v = ev
        for b in reads:
            if b.readers.get(s, 0) < v:
                b.readers[s] = v
        for b in writes:
            if acc:
                if b.writers.get(s, 0) < v:
                    b.writers[s] = v
            else:
                b.writers = {s: v}
                b.readers = {}

    def op(self, e, fn, reads=(), writes=(), acc=False):
        reads = bufs_of(reads)
        writes = bufs_of(writes)
        self._wait(e, self._deps(reads, writes, acc))
        ins = fn()
        self.cnt[e] += 1
        ins.then_inc(self.sem[e], 1)
        self._commit((f"s_{e}", self.cnt[e]), reads, writes, acc)
        self.ninstr += 1
        return ins

    def dma(self, out, in_, q=None, acc=False):
        if q is None:
            q = "sp"
        reads = [in_.buf]
        writes = [out.buf]
        self._wait(q, self._deps(reads, writes, acc))
        owner = out.buf if out.buf.space == "sb" else (in_.buf if in_.buf.space == "sb" else None)
        if owner is None:
            self.dd_cnt += 16
            sem, sname, val = self.dd_sem, "s_dd", self.dd_cnt
        else:
            if owner.dsem is None:
                owner.dsem = self.nc.alloc_semaphore(name=f"d_{owner.name}")
                self.semobj[f"d_{owner.name}"] = owner.dsem
            owner.dcnt += 16
            sem, sname, val = owner.dsem, f"d_{owner.name}", owner.dcnt
        ins = self.eng[q].dma_start(out=out.ap, in_=in_.ap)
        ins.then_inc(sem, 16)
        self._commit((sname, val), reads, writes, acc)
        self.ninstr += 1
        return ins

    def finish(self, bufs):
        deps = {}
        for b in bufs:
            for s, v in b.writers.items():
                if deps.get(s, 0) < v:
                    deps[s] = v
        self._wait("sp", deps)

    def alt(self):
        self.rr += 1
        return "dve" if self.rr % 2 else "act"

    def tt(self, out, a, b, op, e="dve"):
        return self.op(e, lambda: self.eng[e].tensor_tensor(out=out.ap, in0=a.ap, in1=b.ap, op=op),
                       reads=[a, b], writes=[out])

    def ts(self, out, a, s1, op0, s2=None, op1=None, e="dve"):
        r = [a] + [s for s in (s1, s2) if isinstance(s, View)]
        g = lambda s: s.ap if isinstance(s, View) else s
        if op1 is None:
            return self.op(e, lambda: self.eng[e].tensor_scalar(out=out.ap, in0=a.ap, scalar1=g(s1), scalar2=None, op0=op0),
                           reads=r, writes=[out])
        return self.op(e, lambda: self.eng[e].tensor_scalar(out=out.ap, in0=a.ap, scalar1=g(s1), scalar2=g(s2), op0=op0, op1=op1),
                       reads=r, writes=[out])

    def stt(self, out, a, s, b, op0, op1):
        r = [a, b] + ([s] if isinstance(s, View) else [])
        sv = s.ap if isinstance(s, View) else s
        return self.op("dve", lambda: self.nc.vector.scalar_tensor_tensor(out=out.ap, in0=a.ap, scalar=sv, in1=b.ap, op0=op0, op1=op1),
                       reads=r, writes=[out])

    def act(self, out, a, func, scale=1.0, bias=None):
        r = [a] + ([bias] if isinstance(bias, View) else [])
        kw = {}
        if bias is not None:
            kw["bias"] = bias.ap if isinstance(bias, View) else bias
        sc = scale.ap if isinstance(scale, View) else scale
        if isinstance(scale, View):
            r.append(scale)
        return self.op("act", lambda: self.nc.scalar.activation(out=out.ap, in_=a.ap, func=func, scale=sc, **kw),
                       reads=r, writes=[out])

    def copy(self, out, a, e=None):
        if e is None:
            e = self.alt()
        if e == "act":
            return self.op("act", lambda: self.nc.scalar.copy(out=out.ap, in_=a.ap), reads=[a], writes=[out])
        return self.op(e, lambda: self.eng[e].tensor_copy(out=out.ap, in_=a.ap), reads=[a], writes=[out])

    def red(self, out, a, op=ALU.add, axis=AX.X):
        return self.op("dve", lambda: self.nc.vector.tensor_reduce(out=out.ap, in_=a.ap, op=op, axis=axis),
                       reads=[a], writes=[out])

    def recip(self, out, a):
        return self.op("dve", lambda: self.nc.vector.reciprocal(out=out.ap, in_=a.ap), reads=[a], writes=[out])

    def memset(self, out, val, e="pool"):
        return self.op(e, lambda: self.eng[e].memset(out.ap, val), writes=[out])

    def mm(self, out, lhsT, rhs, start=True, stop=True):
        return self.op("pe", lambda: self.nc.tensor.matmul(out.ap, lhsT=lhsT.ap, rhs=rhs.ap, start=start, stop=stop),
                       reads=[lhsT, rhs], writes=[out], acc=not start)

    def tr(self, out, a, ident):
        n = a.ap.shape[0]
        return self.op("pe", lambda: self.nc.tensor.transpose(out.ap, a.ap, ident.ap[:n, :n]),
                       reads=[a, ident], writes=[out], acc=True)


def bcast_rows(view, nrows):
    ap = view.ap
    if len(ap.shape) == 1:
        ap = ap.rearrange("(o n) -> o n", o=1)
    return View(view.buf, ap.to_broadcast([nrows, ap.shape[-1]]))


class Prog:
    def __init__(self, nlayers=DEPTH, dbg=False, mixtest=None):
        self.nlayers = nlayers
        self.dbg = dbg
        self.mixtest = mixtest
        self.es = ExitStack()
        nc = self.nc = bass.Bass("TRN2", target_bir_lowering=False)
        K = self.K = KB(nc, self.es)
        inp = lambda n, s: K.dram(n, s, "ExternalInput")
        outp = lambda n, s: K.dram(n, s, "ExternalOutput")
        self.x_p = inp("x_p", [2048, D])
        self.x_s = inp("x_s", [128, D])
        self.meta = inp("meta_tokens", [NMETA, D])
        self.st_wkv = inp("st_wkv", [DEPTH, NS, 8, 64, 64])
        self.st_shift = inp("st_shift", [DEPTH, NS, RW_COLS])
        self.st_ret = inp("st_ret", [DEPTH, NS, 4, 64, 128])
        self.st_hg = inp("st_hg", [DEPTH, NS, 4, 128, 128])
        self.st_gd = inp("st_gd", [DEPTH, NS, 4, 128, 128])
        self.st_conv = inp("st_conv", [DEPTH, NS, 3, 1536])
        self.W = {}
        for n, s in [("norm_mix", [DEPTH, D]), ("w_in", [DEPTH, D, IN_COLS]), ("rw_mu", [DEPTH, RW_COLS]),
                     ("rw_w0", [DEPTH, 512]), ("rw_w2", [DEPTH, 64, 512]), ("rw_a0", [DEPTH, 512]),
                     ("rw_a2", [DEPTH, 64, 512]), ("rw_g2", [DEPTH, 128, 512]), ("rw_kk", [DEPTH, 512]),
                     ("rw_ka", [DEPTH, 512]), ("rw_rk", [DEPTH, 512]), ("rw_ln_g", [DEPTH, 512]),
                     ("rw_ln_b", [DEPTH, 512]), ("hg_lb", [DEPTH, 512]), ("hg_norm_g", [DEPTH, 512]),
                     ("gd_conv", [DEPTH, 4, 1536]), ("gd_a_log", [DEPTH, 4]), ("gd_dt_bias", [DEPTH, 4]),
                     ("gd_norm_g", [DEPTH, 512]), ("w_branch", [DEPTH, 4 * 512, D]), ("w_out", [DEPTH, D, D]),
                     ("norm_ffn", [DEPTH, D]), ("w_up", [DEPTH, D, 2 * FF]), ("w_down", [DEPTH, FF, D]),
                     ("norm_final", [1, D])]:
            self.W[n] = inp(n, s)
        self.y_p = outp("y_p", [2048, D])
        self.y_s = outp("y_s", [128, D])
        self.o_p = {"wkv": outp("p_wkv", [DEPTH, 8, 64, 64]), "shift": outp("p_shift", [DEPTH, RW_COLS]),
                    "ret": outp("p_ret", [DEPTH, 4, 64, 128]), "hg": outp("p_hg", [DEPTH, 4, 128, 128]),
                    "gd": outp("p_gd", [DEPTH, 4, 128, 128]), "conv": outp("p_conv", [DEPTH, 3, 1536])}
        self.o_s = {"wkv": outp("s_wkv", [DEPTH, NS, 8, 64, 64]), "shift": outp("s_shift", [DEPTH, NS, RW_COLS]),
                    "ret": outp("s_ret", [DEPTH, NS, 4, 64, 128]), "hg": outp("s_hg", [DEPTH, NS, 4, 128, 128]),
                    "gd": outp("s_gd", [DEPTH, NS, 4, 128, 128]), "conv": outp("s_conv", [DEPTH, NS, 3, 1536])}
        self.RESID = K.dram("RESID", [TPAD, D])
        if mixtest:
            self.PROJ = inp("PROJ", [TPAD, IN_COLS])
        else:
            self.PROJ = K.dram("PROJ", [TPAD, IN_COLS]) if not dbg else outp("PROJ", [TPAD, IN_COLS])
        self.OB = K.dram("OB", [TPAD, D]) if not dbg else outp("OB", [TPAD, D])
        self.ACT = K.dram("ACTS", [TPAD, FF])
        if dbg:
            self.XDBG = outp("XDBG", [TPAD, D])
        self.WB = [K.sb(f"WB{i}", [128, 8192]) for i in range(2)]
        self.FT = [K.sb(f"FT{i}", [128, 2048]) for i in range(6)]
        self.xt = [K.sb(f"xt{i}", [128, D]) for i in range(2)]
        self.ht = K.sb("ht", [128, D])
        self.ot = [K.sb(f"ot{i}", [128, 512]) for i in range(3)]
        self.gB = K.sb("gB", [128, D])
        self.ident = K.sb("ident", [128, 128])
        self.sm = [K.sb(f"sm{i}", [128, 8]) for i in range(4)]
        self.PS = [K.ps(f"PS{i}", [128, 512]) for i in range(8)]
        self.pscnt = 0
        self.otcnt = 0
        self.FTB = []
        for i in range(6):
            v = self.FT[i].a.bitcast(BF16)
            for hf in range(2):
                self.FTB.append(self.FT[i].alias(f"FTB{i}_{hf}", v[:, hf * 2048:(hf + 1) * 2048]))
        self.FTW = [self.FT[i].alias(f"FTW{i}", self.FT[i].a.bitcast(BF16)) for i in range(6)]
        self.WBB = []
        for i in range(2):
            v = self.WB[i].a.bitcast(BF16)
            for hf in range(2):
                self.WBB.append(self.WB[i].alias(f"WBB{i}_{hf}", v[:, hf * 8192:(hf + 1) * 8192]))
        self.WBW = [self.WB[i].alias(f"WBW{i}", self.WB[i].a.bitcast(BF16)) for i in range(2)]
        mixer_init(self)

    def rows(self, ti):
        r0 = ti * 128
        return r0, min(128, T - r0)

    def nextps(self, lo=0, hi=2):
        self.pscnt += 1
        return self.PS[lo + self.pscnt % (hi - lo)]

    def rmsnorm_tile(self, xt, n, gB, out):
        K = self.K
        sm = self.sm[0]
        K.op("act", lambda: self.nc.scalar.activation(out=out.a[:n], in_=xt.a[:n], func=AF.Square,
                                                       accum_out=sm.a[:n, 0:1]),
             reads=[xt], writes=[out, sm])
        K.ts(sm[:n, 1:2], sm[:n, 0:1], 1.0 / D, ALU.mult, EPS, ALU.add)
        K.act(sm[:n, 2:3], sm[:n, 1:2], AF.Sqrt)
        K.recip(sm[:n, 3:4], sm[:n, 2:3])
        K.stt(out[:n], xt[:n], sm[:n, 3:4], gB[:n], ALU.mult, ALU.mult)

    def to_featT(self, src, n, ft, nk):
        K = self.K
        for j in range(0, nk, 4):
            ps = self.nextps(2, 4)
            m = min(4, nk - j)
            for q in range(m):
                K.tr(ps[:, q * 128:q * 128 + n], src[:n, (j + q) * 128:(j + q + 1) * 128], self.ident.v)
            K.copy(ft[:, j * 128:(j + m) * 128].re("p (k t) -> p k t", t=128)[:, :, :n],
                   ps[:, :m * 128].re("p (k t) -> p k t", t=128)[:, :, :n])

    def wload(self, slot, wview, nk, c0, cw):
        self.K.dma(slot[:, :nk * cw].re("p (k c) -> p k c", c=cw),
                   wview.re("(k p) c -> p k c", p=128)[:, :, c0:c0 + cw], q="pool")

    def dense(self, nk, ncols, blk, wview, feat_loader, epilogue, G):
        K = self.K
        nb = (ncols + blk - 1) // blk
        blocks = [(b * blk, min(blk, ncols - b * blk)) for b in range(nb)]
        wcnt = 0
        for g0 in range(0, NT, G):
            tiles = list(range(g0, min(g0 + G, NT)))
            for i, ti in enumerate(tiles):
                feat_loader(ti, self.FTB[i])
            self.wload(self.WBB[wcnt % 4], wview, nk, *blocks[0])
            for b, (c0, cw) in enumerate(blocks):
                wb = self.WBB[wcnt % 4]
                wcnt += 1
                if b + 1 < nb:
                    self.wload(self.WBB[wcnt % 4], wview, nk, *blocks[b + 1])
                for i, ti in enumerate(tiles):
                    r0, n = self.rows(ti)
                    ps = self.nextps(0, 2)
                    for k in range(nk):
                        K.mm(ps[:n, :cw], self.FTB[i][:, k * 128:k * 128 + n], wb[:, k * cw:(k + 1) * cw],
                             start=(k == 0), stop=(k == nk - 1))
                    epilogue(ti, b, c0, cw, ps)

    def next_ot(self):
        self.otcnt += 1
        return self.ot[self.otcnt % 3]

    def setup(self):
        K = self.K
        nc = self.nc
        K.memset(self.ident.v, 1.0)
        K.op("pool", lambda: nc.gpsimd.affine_select(out=self.ident.a, in_=self.ident.a, pattern=[[-1, 128]],
                                                      compare_op=ALU.is_equal, fill=0.0, base=0, channel_multiplier=1),
             reads=[self.ident.v], writes=[self.ident.v])
        K.dma(self.RESID[0:128, :], self.x_s.v, q="sp")
        K.dma(self.RESID[128:128 + NMETA, :], self.meta.v, q="sp", acc=True)
        K.dma(self.RESID[128 + NMETA:T, :], self.x_p.v, q="sp", acc=True)

    def phase_inproj(self, l):
        K = self.K
        K.dma(self.gB.v, bcast_rows(self.W["norm_mix"][l], 128))

        def loader(ti, ft):
            r0, n = self.rows(ti)
            xt = self.xt[ti % 2]
            K.dma(xt[:n], self.RESID[r0:r0 + n, :])
            self.rmsnorm_tile(xt, n, self.gB, self.ht)
            self.to_featT(self.ht, n, ft, 16)

        def epi(ti, b, c0, cw, ps):
            r0, n = self.rows(ti)
            ot = self.next_ot()
            K.copy(ot[:n, :cw], ps[:n, :cw])
            K.dma(self.PROJ[r0:r0 + n, c0:c0 + cw], ot[:n, :cw], acc=True)

        self.dense(16, IN_COLS, 512, self.W["w_in"][l], loader, epi, G=9)

    def phase_mixers_stub(self, l):
        K = self.K
        z = self.next_ot()
        K.memset(z.v, 0.0)
        for ti in range(NT):
            r0, n = self.rows(ti)
            for j in range(4):
                K.dma(self.OB[r0:r0 + n, j * 512:(j + 1) * 512], z[:n, :], acc=True)

    def phase_merge_out(self, l):
        K = self.K
        wbr = self.W["w_branch"][l]
        mg = self.ht
        first = {}

        def loader(ti, ft):
            r0, n = self.rows(ti)
            xt = self.xt[ti % 2]
            K.dma(xt[:n], self.OB[r0:r0 + n, :])
            self.to_featT(xt, n, ft, 16)

        nk = 4
        GM = 6
        for g0 in range(0, NT, GM):
            tiles = list(range(g0, min(g0 + GM, NT)))
            for i, ti in enumerate(tiles):
                loader(ti, self.FTB[i])
            items = [(cb, br) for cb in range(4) for br in range(4)]
            wv = lambda cb, br: wbr[br * 512:(br + 1) * 512, :]
            wc = 0
            self.wload(self.WBB[0], wv(0, 0), nk, 0, 512)
            for it, (cb, br) in enumerate(items):
                c0 = cb * 512
                wb = self.WBB[wc % 4]
                wc += 1
                if it + 1 < len(items):
                    ncb, nbr = items[it + 1]
                    self.wload(self.WBB[wc % 4], wv(ncb, nbr), nk, ncb * 512, 512)
                for i, ti in enumerate(tiles):
                    r0, n = self.rows(ti)
                    ps = self.nextps(0, 2)
                    for k in range(nk):
                        K.mm(ps[:n, :], self.FTB[i][:, (br * 4 + k) * 128:(br * 4 + k) * 128 + n],
                             wb[:, k * 512:(k + 1) * 512], start=(k == 0), stop=(k == nk - 1))
                    gt = self.next_ot()
                    K.dma(gt[:n, :], self.PROJ[r0:r0 + n, OFF_G + br * D + c0:OFF_G + br * D + c0 + 512])
                    K.act(gt[:n, :], gt[:n, :], AF.Sigmoid)
                    acc = self.mgacc[i]
                    if br == 0:
                        K.tt(acc[:n, :], gt[:n, :], ps[:n, :], ALU.mult)
                    else:
                        K.tt(gt[:n, :], gt[:n, :], ps[:n, :], ALU.mult)
                        K.tt(acc[:n, :], acc[:n, :], gt[:n, :], ALU.add)
                    if br == 3:
                        K.dma(self.ACT[r0:r0 + n, c0:c0 + 512], acc[:n, :], acc=True)

        def loader2(ti, ft):
            r0, n = self.rows(ti)
            xt = self.xt[ti % 2]
            K.dma(xt[:n], self.ACT[r0:r0 + n, 0:D])
            self.to_featT(xt, n, ft, 16)

        def epi2(ti, b, c0, cw, ps):
            r0, n = self.rows(ti)
            ot = self.next_ot()
            K.dma(ot[:n, :cw], self.RESID[r0:r0 + n, c0:c0 + cw])
            K.tt(ot[:n, :cw], ot[:n, :cw], ps[:n, :cw], ALU.add)
            K.dma(self.RESID[r0:r0 + n, c0:c0 + cw], ot[:n, :cw], acc=True)

        self.dense(16, D, 512, self.W["w_out"][l], loader2, epi2, G=9)

    def phase_ffn(self, l):
        K = self.K
        K.dma(self.gB.v, bcast_rows(self.W["norm_ffn"][l], 128))
        wup = self.W["w_up"][l]

        def loader(ti, ft):
            r0, n = self.rows(ti)
            xt = self.xt[ti % 2]
            K.dma(xt[:n], self.RESID[r0:r0 + n, :])
            self.rmsnorm_tile(xt, n, self.gB, self.ht)
            self.to_featT(self.ht, n, ft, 16)

        nk = 16
        GU = 9
        wupv = wup.re("(k p) c -> p k c", p=128)
        nbu = FF // 512
        for g0 in range(0, NT, GU):
            tiles = list(range(g0, min(g0 + GU, NT)))
            for i, ti in enumerate(tiles):
                loader(ti, self.FTB[i])

            def wl(b, par):
                for half in range(2):
                    c0 = half * FF + b * 512
                    K.dma(self.WBB[par * 2 + half][:, :nk * 512].re("p (k c) -> p k c", c=512),
                          wupv[:, :, c0:c0 + 512], q="pool")
            wl(0, 0)
            for b in range(nbu):
                c0 = b * 512
                par = b % 2
                if b + 1 < nbu:
                    wl(b + 1, 1 - par)
                wu, wg = self.WBB[par * 2], self.WBB[par * 2 + 1]
                for i, ti in enumerate(tiles):
                    r0, n = self.rows(ti)
                    pu, pg = self.PS[(i % 2) * 2], self.PS[(i % 2) * 2 + 1]
                    for k in range(nk):
                        K.mm(pu[:n, :], self.FTB[i][:, k * 128:k * 128 + n], wu[:, k * 512:(k + 1) * 512],
                             start=(k == 0), stop=(k == nk - 1))
                    for k in range(nk):
                        K.mm(pg[:n, :], self.FTB[i][:, k * 128:k * 128 + n], wg[:, k * 512:(k + 1) * 512],
                             start=(k == 0), stop=(k == nk - 1))
                    ot = self.next_ot()
                    K.act(ot[:n, :], pg[:n, :], AF.Silu)
                    K.tt(ot[:n, :], ot[:n, :], pu[:n, :], ALU.mult)
                    K.dma(self.ACT[r0:r0 + n, c0:c0 + 512], ot[:n, :], acc=True)

        wdn = self.W["w_down"][l]
        nk2 = FF // 128
        GD_ = 3
        blkd = 256
        nbd = D // blkd
        for g0 in range(0, NT, GD_):
            tiles = list(range(g0, min(g0 + GD_, NT)))
            for i, ti in enumerate(tiles):
                r0, n = self.rows(ti)
                for part in range(3):
                    kk0 = part * 16
                    m = min(16, nk2 - kk0)
                    xt = self.xt[part % 2]
                    K.dma(xt[:n, :m * 128], self.ACT[r0:r0 + n, kk0 * 128:(kk0 + m) * 128])
                    ftw = self.FTW[i * 2 + (part // 2)]
                    self.to_featT(xt, n, ftw[:, (part % 2) * 2048:(part % 2) * 2048 + m * 128], m)
            self.wload(self.WBW[0], wdn, nk2, 0, blkd)
            for b in range(nbd):
                c0 = b * blkd
                wb = self.WBW[b % 2]
                if b + 1 < nbd:
                    self.wload(self.WBW[(b + 1) % 2], wdn, nk2, (b + 1) * blkd, blkd)
                for i, ti in enumerate(tiles):
                    r0, n = self.rows(ti)
                    ps = self.nextps(0, 2)
                    for k in range(nk2):
                        ftw = self.FTW[i * 2 + (k // 32)]
                        kk = k % 32
                        K.mm(ps[:n, :blkd], ftw[:, kk * 128:kk * 128 + n], wb[:, k * blkd:(k + 1) * blkd],
                             start=(k == 0), stop=(k == nk2 - 1))
                    ot = self.next_ot()
                    K.dma(ot[:n, :blkd], self.RESID[r0:r0 + n, c0:c0 + blkd])
                    K.tt(ot[:n, :blkd], ot[:n, :blkd], ps[:n, :blkd], ALU.add)
                    K.dma(self.RESID[r0:r0 + n, c0:c0 + blkd], ot[:n, :blkd], acc=True)

    def phase_final(self):
        K = self.K
        K.dma(self.gB.v, bcast_rows(self.W["norm_final"][0], 128))
        for ti in range(NT):
            r0, n = self.rows(ti)
            xt = self.xt[ti % 2]
            K.dma(xt[:n], self.RESID[r0:r0 + n, :])
            if self.dbg:
                K.dma(self.XDBG[r0:r0 + n, :], xt[:n], acc=True)
            self.rmsnorm_tile(xt, n, self.gB, self.ht)
            if ti == 0:
                K.dma(self.y_s.v, self.ht.v, acc=True)
            elif ti == 1:
                K.dma(self.y_p[0:128 - NMETA, :], self.ht[NMETA:128, :], acc=True)
            else:
                p0 = (ti - 1) * 128 - NMETA
                K.dma(self.y_p[p0:p0 + n, :], self.ht[:n, :], acc=True)

    def build(self):
        K = self.K
        self.mgacc = self.mx[0:6]
        self.setup()
        mixer_setup(self)
        if self.mixtest:
            for m in self.mixtest:
                MIXERS[m](self, 0)
            outs = [self.OB] + list(self.o_p.values()) + list(self.o_s.values())
            K.finish(outs)
            return self.nc
        for l in range(self.nlayers):
            self.phase_inproj(l)
            self.phase_mixers(l)
            self.phase_merge_out(l)
            self.phase_ffn(l)
        self.phase_final()
        outs = [self.y_p, self.y_s] + list(self.o_p.values()) + list(self.o_s.values())
        if self.dbg:
            outs += [self.PROJ, self.OB, self.XDBG]
        K.finish(outs)
        return self.nc

    def phase_mixers(self, l):
        mix_rw(self, l)
        mix_ret(self, l)
        mix_hg(self, l)
        mix_gd(self, l)


WNAMES = ["norm_mix", "w_in", "rw_mu", "rw_w0", "rw_w2", "rw_a0", "rw_a2", "rw_g2", "rw_kk", "rw_ka", "rw_rk",
          "rw_ln_g", "rw_ln_b", "hg_lb", "hg_norm_g", "gd_conv", "gd_a_log", "gd_dt_bias", "gd_norm_g",
          "w_branch", "w_out", "norm_ffn", "w_up", "w_down", "norm_final"]


def make_in_maps(inputs, ncores=NCORE):
    f = lambda a: np.ascontiguousarray(np.asarray(a, dtype=np.float32))
    shared = {}
    for n in WNAMES:
        a = f(inputs[n])
        if n == "w_branch":
            a = a.reshape(DEPTH, 4 * 512, D)
        elif n == "rw_rk":
            a = a.reshape(DEPTH, 512)
        elif n == "norm_final":
            a = a.reshape(1, D)
        shared[n] = a
    shared["meta_tokens"] = f(inputs["meta_tokens"])
    shared["consts"], shared["ebc"], shared["rot"] = make_consts()
    maps = []
    for c in range(ncores):
        m = dict(shared)
        m["x_p"] = f(inputs["x_prompt"][c % 4])
        sl = slice(c * NS, (c + 1) * NS)
        m["x_s"] = f(inputs["x_sample"][sl]).reshape(128, D)
        m["st_wkv"] = f(inputs["state_rwkv_wkv"][:, sl])
        m["st_shift"] = f(inputs["state_rwkv_shift"][:, sl])
        m["st_ret"] = f(inputs["state_ret"][:, sl])
        m["st_hg"] = f(inputs["state_hgrn"][:, sl])
        m["st_gd"] = f(inputs["state_gdn"][:, sl])
        m["st_conv"] = f(inputs["state_gdn_conv"][:, sl])
        maps.append(m)
    return maps


def kernel(**inputs):
    prog = Prog()
    nc = prog.build()
    maps = make_in_maps(inputs)
    res = run_bass_kernel_spmd(nc, maps, core_ids=list(range(NCORE)))
    R = res.results
    y_prompt = np.stack([R[b]["y_p"] for b in range(4)], axis=0)
    y_sample = np.concatenate([R[c]["y_s"].reshape(NS, TS, D) for c in range(NCORE)], axis=0)
    outs = [y_prompt, y_sample]
    for k in ["wkv", "shift", "ret", "hg", "gd", "conv"]:
        outs.append(np.stack([R[b]["p_" + k] for b in range(4)], axis=1))
    for k in ["wkv", "shift", "ret", "hg", "gd", "conv"]:
        outs.append(np.concatenate([R[c]["s_" + k] for c in range(NCORE)], axis=1))
    return tuple(np.ascontiguousarray(o, dtype=np.float32) for o in outs)


NCONST = 2048
C_MU_I, C_MU_S, C_ML_S, C_MS_I, C_MS_S, C_MSL_S, C_ID, C_ET, C_GP, C_GS, C_IND = 0, 128, 256, 384, 512, 640, 768, 896, 912, 920, 928
C_MID = 936
C_ONES, C_NI, C_NLS, C_NSI, C_NSLS = 1024, 1152, 1280, 1408, 1536


def make_consts():
    c = np.zeros((128, NCONST), np.float32)
    s = np.arange(128)[:, None]
    t = np.arange(128)[None, :]
    same = (s // 8) == (t // 8)
    c[:, C_MU_I:C_MU_I + 128] = (s <= t)
    c[:, C_MU_S:C_MU_S + 128] = (s < t)
    c[:, C_ML_S:C_ML_S + 128] = (t < s)
    c[:, C_MS_I:C_MS_I + 128] = (s <= t) & same
    c[:, C_MS_S:C_MS_S + 128] = (s < t) & same
    c[:, C_MSL_S:C_MSL_S + 128] = (t < s) & same
    c[:, C_ID:C_ID + 128] = (s == t)
    c[:, C_ET:C_ET + 16] = (np.arange(128)[:, None] // 8) == np.arange(16)[None, :]
    lg = np.log1p(-np.exp2(-5.0 - np.arange(4, dtype=np.float64)))
    i = np.arange(128, dtype=np.float64)[:, None]
    c[:, C_GP:C_GP + 4] = np.exp((i + 1) * lg[None])
    c[:, C_GP + 4:C_GP + 8] = np.exp(-(i + 1) * lg[None]) / 8.0
    i8 = (np.arange(128) % 8).astype(np.float64)[:, None]
    c[:, C_GS:C_GS + 4] = np.exp((i8 + 1) * lg[None])
    c[:, C_GS + 4:C_GS + 8] = np.exp(-(i8 + 1) * lg[None]) / 8.0
    c[:, C_IND] = (np.arange(128) % 64) <= 31
    c[:, C_IND + 1] = 1.0
    c[:64, C_MID:C_MID + 64] = (s[:64] <= t[:, :64]).astype(np.float32) - (np.arange(64)[:, None] <= 31).astype(np.float32)
    c[:, C_ONES:C_ONES + 128] = 1.0
    NEG = -30000.0
    c[:, C_NI:C_NI + 128] = np.where(s <= t, 0.0, NEG)
    c[:, C_NLS:C_NLS + 128] = np.where(t < s, 0.0, NEG)
    c[:, C_NSI:C_NSI + 128] = np.where((s <= t) & same, 0.0, NEG)
    c[:, C_NSLS:C_NSLS + 128] = np.where((t < s) & same, 0.0, NEG)
    eb = np.zeros((128, 16, 128), np.float32)
    eb[:] = ((np.arange(128)[None, :] // 8) == np.arange(16)[:, None])[None]
    rot = np.zeros((TPAD, 64), np.float32)
    pos = np.zeros(TPAD, np.float32)
    pos[0:128] = 16384 + (np.arange(128) % 8)
    pos[128:T] = np.arange(TP)
    inv = (np.float32(10000.0) ** (-np.arange(32, dtype=np.float32) / np.float32(32))).astype(np.float32)
    ang = (pos[:, None].astype(np.float32) * inv[None, :]).astype(np.float32)
    rot[:, :32] = np.cos(ang.astype(np.float64))
    rot[:, 32:] = np.sin(ang.astype(np.float64))
    return c, eb.reshape(128, 2048), rot


RET_LG = [float(np.log1p(-np.exp2(-5.0 - h))) for h in range(4)]


def mixer_init(self):
    K = self.K
    self.CONST = K.dram("consts", [128, NCONST], "ExternalInput")
    self.EBD = K.dram("ebc", [128, 2048], "ExternalInput")
    self.ROT = K.dram("rot", [TPAD, 64], "ExternalInput")
    self.cst = K.sb("cst", [128, NCONST])
    self.Eb = K.sb("Eb", [128, 2048])
    self.rot_t = K.sb("rot_t", [128, 64])
    self.mx = [K.sb(f"mx{i}", [128, 512]) for i in range(8)]
    self.hd = [K.sb(f"hd{i}", [128, 128]) for i in range(NLANE * LSTR + 4)]
    self.CONVS = K.dram("CONVS", [NS, 11, 1536])
    self.SHS = K.dram("SHS", [NS, 9, RW_COLS])
    self.Hst = K.sb("Hst", [128, 4 * 128])
    self.sm2 = K.sb("smx2", [128, 32])


def mixer_setup(self):
    K = self.K
    K.dma(self.cst.v, self.CONST.v)
    K.dma(self.Eb.v, self.EBD.v)
    K.copy(self.ident.v, self.cst[:, C_ID:C_ID + 128], e="dve")


NLANE = 2
LSTR = 20


def mps(self):
    self.pscnt += 1
    return self.PS[self.pscnt % 8]


def rr(gens):
    gens = list(gens)
    while gens:
        for g in list(gens):
            try:
                next(g)
            except StopIteration:
                gens.remove(g)


def run_heads(head, nh, smp):
    if smp:
        for h in range(nh):
            for _ in head(h, 0):
                pass
    else:
        for h0 in range(0, nh, NLANE):
            rr([head(h0 + L, L) for L in range(NLANE) if h0 + L < nh])


def headT(self, dst, src, n, dk):
    K = self.K
    ps = mps(self)
    K.tr(ps[:dk, :n], src, self.ident.v)
    K.copy(dst, ps[:dk, :n])


def rms_heads_gate(self, o, n, nh, dv, gate, gain, outv, r0, ocol):
    K = self.K
    sq = self.mx[7]
    sm = self.sm2
    K.tt(sq[:n, :nh * dv], o[:n, :nh * dv], o[:n, :nh * dv], ALU.mult)
    K.red(sm[:n, 0:nh], sq[:n, :nh * dv].re("p (h d) -> p h d", d=dv))
    K.ts(sm[:n, 8:8 + nh], sm[:n, 0:nh], 1.0 / dv, ALU.mult, EPS, ALU.add)
    K.act(sm[:n, 16:16 + nh], sm[:n, 8:8 + nh], AF.Sqrt)
    K.recip(sm[:n, 24:24 + nh], sm[:n, 16:16 + nh])
    for h in range(nh):
        K.ts(o[:n, h * dv:(h + 1) * dv], o[:n, h * dv:(h + 1) * dv], sm[:n, 24 + h:25 + h], ALU.mult)
    if gain is not None:
        K.tt(o[:n, :nh * dv], o[:n, :nh * dv], gain[:n, :nh * dv], ALU.mult)
    K.act(sq[:n, :512], gate, AF.Silu)
    K.tt(o[:n, :512], o[:n, :512], sq[:n, :512], ALU.mult)
    K.dma(self.OB[r0:r0 + n, ocol:ocol + 512], o[:n, :512], acc=True)


def lin_core(self, n, dk, dv, kT, qT, mask, qT_state, Hs, X, ktok, o_out, upd, ST=None):
    K = self.K
    ps = mps(self)
    K.mm(ps[:n, :n], kT, qT)
    if ST is None:
        ST = self.hd[9]
    K.tt(ST[:n, :n], ps[:n, :n], mask, ALU.mult)
    yield
    po = mps(self)
    for j in range(len(Hs)):
        K.mm(po[:n, :dv], qT_state[j], Hs[j], start=(j == 0), stop=False)
    K.mm(po[:n, :dv], ST[:n, :n], X, start=False, stop=True)
    K.copy(o_out, po[:n, :dv])
    yield
    for j0 in range(0, len(Hs), 4):
        pu = mps(self)
        js = list(range(j0, min(j0 + 4, len(Hs))))
        for q, j in enumerate(js):
            K.mm(pu[:dk, q * dv:(q + 1) * dv], ktok[j], X)
        for q, j in enumerate(js):
            upd(Hs[j], pu[:dk, q * dv:(q + 1) * dv])


def mix_ret(self, l):
    K = self.K
    pa = self.xt[0]
    o_all = self.mx[0]
    Hst = self.Hst
    K.memset(Hst.v, 0.0, e="dve")
    SH = self.FT[0]
    for ti in range(NT):
        r0, n = self.rows(ti)
        smp = (ti == 0)
        K.dma(pa[:n, :RET_COLS], self.PROJ[r0:r0 + n, OFF_RET:OFF_RET + RET_COLS])
        K.dma(self.rot_t[:n, :], self.ROT[r0:r0 + n, :])
        cos = self.rot_t[:n, 0:32]
        sin = self.rot_t[:n, 32:64]
        qk = self.mx[1]
        tmp = self.mx[2]
        for w in range(2):
            src = pa[:n, w * 256:(w + 1) * 256].re("p (h two d) -> p h two d", two=2, d=32)
            dst = qk[:n, w * 256:(w + 1) * 256].re("p (h two d) -> p h two d", two=2, d=32)
            tv = tmp[:n, 0:256].re("p (h two d) -> p h two d", two=2, d=32)
            for h in range(4):
                K.tt(dst[:, h, 0, :], src[:, h, 0, :], cos, ALU.mult)
                K.tt(tv[:, h, 0, :], src[:, h, 1, :], sin, ALU.mult)
                K.tt(dst[:, h, 0, :], dst[:, h, 0, :], tv[:, h, 0, :], ALU.subtract)
                K.tt(dst[:, h, 1, :], src[:, h, 0, :], sin, ALU.mult)
                K.tt(tv[:, h, 1, :], src[:, h, 1, :], cos, ALU.mult)
                K.tt(dst[:, h, 1, :], dst[:, h, 1, :], tv[:, h, 1, :], ALU.add)
        gcol = C_GS if smp else C_GP
        mask = self.cst[:n, (C_MS_I if smp else C_MU_I):(C_MS_I if smp else C_MU_I) + n]
        for h in range(4):
            K.ts(qk[:n, h * 64:(h + 1) * 64], qk[:n, h * 64:(h + 1) * 64], self.cst[:n, gcol + h:gcol + h + 1], ALU.mult)
            K.ts(qk[:n, 256 + h * 64:256 + (h + 1) * 64], qk[:n, 256 + h * 64:256 + (h + 1) * 64],
                 self.cst[:n, gcol + 4 + h:gcol + 5 + h], ALU.mult)
        def head(h, L, n=n, smp=smp, mask=mask):
            qT, kT, STb = self.hd[L * LSTR], self.hd[L * LSTR + 1], self.hd[L * LSTR + 2]
            headT(self, qT[:64, :n], qk[:n, h * 64:(h + 1) * 64], n, 64)
            headT(self, kT[:64, :n], qk[:n, 256 + h * 64:256 + (h + 1) * 64], n, 64)
            X = pa[:n, 512 + h * 128:512 + (h + 1) * 128]
            ktok = qk[:n, 256 + h * 64:256 + (h + 1) * 64]
            if not smp:
                gC = float(np.exp(n * RET_LG[h]))

                def upd(H, psv, gC=gC):
                    K.tt(H, H, psv, ALU.add)
                    K.ts(H, H, gC, ALU.mult)
                yield from lin_core(self, n, 64, 128, kT[:64, :n], qT[:64, :n], mask, [qT[:64, :n]],
                                    [Hst[:64, h * 128:(h + 1) * 128]], X, [ktok], o_all[:n, h * 128:(h + 1) * 128], upd, ST=STb)
            else:
                gC = float(np.exp(TS * RET_LG[h]))
                K.dma(SH[:64, :].re("p (s v) -> p s v", v=128), self.st_ret[l, :, h].re("s k v -> k s v"))
                Qm = self.FT[1]
                Km = self.FT[2]
                K.tt(Qm[:64, :].re("p (j t) -> p j t", t=128), self.Eb[:64, :].re("p (j t) -> p j t", t=128),
                     View(qT, qT.a[:64, :].rearrange("p (o t) -> p o t", o=1).to_broadcast([64, 16, 128])), ALU.mult)
                kb = View(ktok.buf, ktok.ap.rearrange("p (o d) -> p o d", o=1).to_broadcast([128, 16, 64]))
                eb = View(self.cst, self.cst.a[:, C_ET:C_ET + 16].rearrange("p (j o) -> p j o", o=1).to_broadcast([128, 16, 64]))
                K.tt(Km[:, :1024].re("p (j d) -> p j d", d=64), kb, eb, ALU.mult)

                def upd(H, psv, gC=gC):
                    K.tt(H, H, psv, ALU.add)
                    K.ts(H, H, gC, ALU.mult)
                yield from lin_core(self, n, 64, 128, kT[:64, :n], qT[:64, :n], mask,
                                    [Qm[:64, j * 128:(j + 1) * 128] for j in range(NS)],
                                    [SH[:64, j * 128:(j + 1) * 128] for j in range(NS)], X,
                                    [Km[:, j * 64:(j + 1) * 64] for j in range(NS)], o_all[:n, h * 128:(h + 1) * 128], upd, ST=STb)
                K.dma(self.o_s["ret"][l, :, h].re("s k v -> k s v"), SH[:64, :].re("p (s v) -> p s v", v=128), acc=True)
        run_heads(head, 4, smp)
        rms_heads_gate(self, o_all, n, 4, 128, pa[:n, 1024:1536], None, None, r0, 512)
    K.dma(self.o_p["ret"][l].re("h k v -> k h v"), Hst[:64, :].re("p (h v) -> p h v", v=128), acc=True)


def mix_hg(self, l):
    K = self.K
    pa = self.xt[0]
    o_all = self.mx[0]
    Hst = self.Hst
    prm = self.xt[1]
    gn = self.mx[6]
    K.dma(prm[:, :].re("p (j c) -> p j c", c=512), View(self.W["hg_lb"], self.W["hg_lb"].a.rearrange("(o j) c -> o j c", o=1).to_broadcast([128, 4, 512])))
    K.act(prm[:, :], prm[:, :], AF.Exp)
    den = self.mx[5]
    K.tt(den[:, :], prm[:, 0:512], prm[:, 512:1024], ALU.add)
    K.tt(den[:, :], den[:, :], prm[:, 1024:1536], ALU.add)
    K.tt(den[:, :], den[:, :], prm[:, 1536:2048], ALU.add)
    K.recip(den[:, :], den[:, :])
    lb = self.mx[4]
    K.memset(lb.v, 0.0, e="dve")
    for j in range(1, l + 1):
        K.tt(lb[:, :], lb[:, :], prm[:, j * 512:(j + 1) * 512], ALU.add)
    K.tt(lb[:, :], lb[:, :], den[:, :], ALU.mult)
    oml = self.mx[5]
    K.ts(oml[:, :], lb[:, :], -1.0, ALU.mult, 1.0, ALU.add)
    K.dma(gn.v, bcast_rows(self.W["hg_norm_g"][l], 128))
    K.memset(Hst.v, 0.0, e="dve")
    SH = self.FT[0]
    chunks = [(0, 128, True)] + [(128 + 64 * i, 64, False) for i in range(32)] + [(128 + 2048, 16, False)]
    for (r0, n, smp) in chunks:
        K.dma(pa[:n, :HG_COLS], self.PROJ[r0:r0 + n, OFF_HG:OFF_HG + HG_COLS])
        sg, logf, kin, cm = self.mx[1], self.mx[2], self.mx[3], self.mx[7]
        K.act(sg[:n, :], pa[:n, 512:1024], AF.Sigmoid)
        K.tt(sg[:n, :], sg[:n, :], oml[:n, :], ALU.mult)
        K.tt(kin[:n, :], oml[:n, :], sg[:n, :], ALU.subtract)
        K.tt(logf[:n, :], sg[:n, :], lb[:n, :], ALU.add)
        K.act(logf[:n, :], logf[:n, :], AF.Ln)
        if smp:
            cmat = self.cst[:n, C_MS_I:C_MS_I + n]
            mask = cmat
        elif n == 64:
            cmat = self.cst[:n, C_MID:C_MID + n]
            mask = self.cst[:n, C_MU_I:C_MU_I + n]
        else:
            cmat = self.cst[:n, C_MU_I:C_MU_I + n]
            mask = cmat
        ps = mps(self)
        K.mm(ps[:n, :512], cmat, logf[:n, :])
        K.act(cm[:n, :], ps[:n, :512], AF.Exp)
        qs = pa
        K.act(sg[:n, :], pa[:n, 0:512], AF.Silu)
        K.tt(qs[:n, 0:512], sg[:n, :], cm[:n, :], ALU.mult)
        K.recip(cm[:n, :], cm[:n, :])
        K.tt(kin[:n, :], kin[:n, :], cm[:n, :], ALU.mult)
        def head(h, L, n=n, smp=smp, mask=mask, r0=r0):
            hc = slice(h * 128, (h + 1) * 128)
            qT, kT, sc, Ht, STb = self.hd[L * LSTR:L * LSTR + 5]
            headT(self, qT[:, :n], qs[:n, hc], n, 128)
            headT(self, kT[:, :n], kin[:n, hc], n, 128)
            X = pa[:n, 1024 + h * 128:1024 + (h + 1) * 128]
            ktok = kin[:n, hc]
            yield
            if not smp:
                pm = mps(self)
                icol = C_IND if n == 64 else C_IND + 2
                K.mm(pm[:, 0:1], logf[:n, hc], self.cst[:n, icol:icol + 1])
                K.mm(pm[:, 1:2], logf[:n, hc], self.cst[:n, C_IND + 1:C_IND + 2])
                K.copy(sc[:, 0:2], pm[:, 0:2], e="dve")
                K.tt(sc[:, 2:3], sc[:, 1:2], sc[:, 0:1], ALU.subtract)
                K.act(sc[:, 3:4], sc[:, 0:1], AF.Exp)
                K.act(sc[:, 4:5], sc[:, 2:3], AF.Exp)
                Hv = Hst[:, hc]
                K.ts(Ht[:, :], Hv, sc[:, 3:4], ALU.mult)

                def upd(H, psv, Hv=Hv, sc=sc):
                    K.tt(Hv, H, psv, ALU.add)
                    K.ts(Hv, Hv, sc[:, 4:5], ALU.mult)
                yield from lin_core(self, n, 128, 128, kT[:, :n], qT[:, :n], mask, [qT[:, :n]], [Ht[:, :]], X, [ktok],
                                    o_all[:n, hc], upd, ST=STb)
            else:
                pm = mps(self)
                K.mm(pm[:, 0:16], logf[:n, hc], self.cst[:n, C_ET:C_ET + 16])
                K.act(sc[:, 0:16], pm[:, 0:16], AF.Exp)
                K.dma(SH[:, :].re("p (s v) -> p s v", v=128), self.st_hg[l, :, h].re("s k v -> k s v"))
                Qm, Km = self.FT[1], self.FT[2]
                K.tt(Qm[:, :].re("p (j t) -> p j t", t=128), self.Eb[:, :].re("p (j t) -> p j t", t=128),
                     View(qT, qT.a[:, :].rearrange("p (o t) -> p o t", o=1).to_broadcast([128, 16, 128])), ALU.mult)
                kb = View(ktok.buf, ktok.ap.rearrange("p (o d) -> p o d", o=1).to_broadcast([128, 16, 128]))
                eb = View(self.cst, self.cst.a[:, C_ET:C_ET + 16].rearrange("p (j o) -> p j o", o=1).to_broadcast([128, 16, 128]))
                K.tt(Km[:, :].re("p (j d) -> p j d", d=128), kb, eb, ALU.mult)
                jmap = {}

                def upd(H, psv, sc=sc, jmap=jmap):
                    j = jmap["j"]
                    jmap["j"] += 1
                    K.tt(H, H, psv, ALU.add)
                    K.ts(H, H, sc[:, j:j + 1], ALU.mult)
                jmap["j"] = 0
                yield from lin_core(self, n, 128, 128, kT[:, :n], qT[:, :n], mask,
                                    [Qm[:, j * 128:(j + 1) * 128] for j in range(NS)],
                                    [SH[:, j * 128:(j + 1) * 128] for j in range(NS)], X,
                                    [Km[:, j * 128:(j + 1) * 128] for j in range(NS)], o_all[:n, hc], upd, ST=STb)
                K.dma(self.o_s["hg"][l, :, h].re("s k v -> k s v"), SH[:, :].re("p (s v) -> p s v", v=128), acc=True)
        run_heads(head, 4, smp)
        rms_heads_gate(self, o_all, n, 4, 128, pa[:n, 1536:2048], gn, None, r0, 1024)
    K.dma(self.o_p["hg"][l].re("h k v -> k h v"), Hst[:, :].re("p (h v) -> p h v", v=128), acc=True)


def solve_unit(self, n, nsteps, MT, M, MTb, Mb, Dl, dv):
    K = self.K
    cur = (MT, M)
    oth = (MTb, Mb)
    for j in range(nsteps):
        pu = mps(self)
        K.mm(pu[:n, :dv], cur[0][:n, :n], Dl)
        K.tt(Dl, Dl, pu[:n, :dv], ALU.add)
        yield
        if j < nsteps - 1:
            p1 = mps(self)
            K.mm(p1[:n, :n], cur[0][:n, :n], cur[1][:n, :n])
            p2 = mps(self)
            K.mm(p2[:n, :n], cur[1][:n, :n], cur[0][:n, :n])
            K.copy(oth[1][:n, :n], p1[:n, :n], e="act")
            K.copy(oth[0][:n, :n], p2[:n, :n], e="dve")
            cur, oth = oth, cur
            yield


def mix_gd(self, l):
    K = self.K
    Hst = self.Hst
    o_all = self.mx[0]
    cw = self.WB[0]
    gn = self.mx[6]
    prm = self.mx[4]
    K.dma(cw[:, :4 * 1536].re("p (j c) -> p j c", c=1536),
          View(self.W["gd_conv"], self.W["gd_conv"].a[l:l + 1].to_broadcast([128, 4, 1536])))
    K.dma(gn.v, bcast_rows(self.W["gd_norm_g"][l], 128))
    K.dma(prm[:, 0:4], bcast_rows(self.W["gd_a_log"][l], 128))
    K.dma(prm[:, 4:8], bcast_rows(self.W["gd_dt_bias"][l], 128))
    K.act(prm[:, 8:12], prm[:, 0:4], AF.Exp)
    K.memset(Hst.v, 0.0, e="dve")
    SH = self.FT[0]
    c0 = OFF_GD
    K.dma(self.CONVS[:, 0:3, :], self.st_conv[l], q="sp")
    K.dma(self.CONVS[:, 3:11, :], self.PROJ[0:128, c0:c0 + 1536].re("(s t) c -> s t c", t=TS), q="sp", acc=True)
    for ti in range(NT):
        r0, n = self.rows(ti)
        smp = (ti == 0)
        x0 = self.FT[3]
        shs = [self.FT[4], self.FT[5], self.WB[1]]
        sm = self.mx[5]
        gate = self.mx[1]
        K.dma(x0[:n, :1536], self.PROJ[r0:r0 + n, c0:c0 + 1536])
        for j in (1, 2, 3):
            sh = shs[j - 1]
            if smp:
                for q in range(NS):
                    K.dma(sh[q * TS:(q + 1) * TS, :1536], self.CONVS[q, 3 - j:11 - j, :], acc=(q > 0))
            else:
                K.dma(sh[:n, :1536], self.PROJ[r0 - j:r0 - j + n, c0:c0 + 1536])
                if ti == 1:
                    K.memset(sh[0:j, :1536], 0.0, e="dve")
        K.dma(sm[:n, 0:8], self.PROJ[r0:r0 + n, c0 + 1536:c0 + 1544])
        K.dma(gate[:n, :], self.PROJ[r0:r0 + n, c0 + 1544:c0 + 2056])
        K.tt(x0[:n, :1536], x0[:n, :1536], cw[:n, 3 * 1536:4 * 1536], ALU.mult)
        for j in (1, 2, 3):
            sh = shs[j - 1]
            K.tt(sh[:n, :1536], sh[:n, :1536], cw[:n, (3 - j) * 1536:(4 - j) * 1536], ALU.mult)
            K.tt(x0[:n, :1536], x0[:n, :1536], sh[:n, :1536], ALU.add)
        K.act(x0[:n, :1536], x0[:n, :1536], AF.Silu)
        sq = self.FT[4]
        K.tt(sq[:n, :1024], x0[:n, :1024], x0[:n, :1024], ALU.mult)
        K.red(sm[:n, 8:16], sq[:n, :1024].re("p (h d) -> p h d", d=128))
        K.ts(sm[:n, 8:16], sm[:n, 8:16], 1e-6, ALU.add)
        K.act(sm[:n, 16:24], sm[:n, 8:16], AF.Sqrt)
        K.recip(sm[:n, 24:32], sm[:n, 16:24])
        K.ts(sm[:n, 24:28], sm[:n, 24:28], float(128 ** -0.5), ALU.mult)
        for h in range(8):
            K.ts(x0[:n, h * 128:(h + 1) * 128], x0[:n, h * 128:(h + 1) * 128], sm[:n, 24 + h:25 + h], ALU.mult)
        K.act(sm[:n, 32:36], sm[:n, 0:4], AF.Sigmoid)
        K.tt(sm[:n, 36:40], sm[:n, 4:8], prm[:n, 4:8], ALU.add)
        K.act(sm[:n, 36:40], sm[:n, 36:40], AF.Exp)
        K.act(sm[:n, 36:40], sm[:n, 36:40], AF.Ln, bias=1.0)
        K.tt(sm[:n, 40:44], sm[:n, 36:40], prm[:n, 8:12], ALU.mult)
        K.ts(sm[:n, 40:44], sm[:n, 40:44], -1.0, ALU.mult)
        mI = C_MS_I if smp else C_MU_I
        mLS = C_MSL_S if smp else C_ML_S
        mUS = C_MS_S if smp else C_MU_S
        nI = C_NSI if smp else C_NI
        nLS = C_NSLS if smp else C_NLS
        pc = mps(self)
        K.mm(pc[:n, 0:4], self.cst[:n, mI:mI + n], sm[:n, 40:44])
        K.mm(pc[:n, 4:8], self.cst[:n, mLS:mLS + n], sm[:n, 40:44])
        K.copy(sm[:n, 44:52], pc[:n, 0:8], e="dve")
        K.ts(sm[:n, 52:56], sm[:n, 44:48], -1.0, ALU.mult)
        K.act(sm[:n, 56:60], sm[:n, 44:48], AF.Exp)
        K.act(sm[:n, 60:64], sm[:n, 48:52], AF.Exp)
        K.tt(sm[:n, 64:68], sm[:n, 56:60], sm[:n, 32:36], ALU.mult)
        K.ts(sm[:n, 64:68], sm[:n, 64:68], -1.0, ALU.mult)
        ecC = self.mx[3]
        pcc = mps(self)
        if not smp:
            K.mm(pcc[:, 0:4], self.cst[:n, C_ONES:C_ONES + 128], sm[:n, 40:44])
            K.act(ecC[:, 0:4], pcc[:, 0:4], AF.Exp)
        else:
            gE = self.mx[2]
            for h in range(4):
                K.ts(gE[:n, h * 16:(h + 1) * 16], self.cst[:n, C_ET:C_ET + 16], sm[:n, 40 + h:41 + h], ALU.mult)
            K.mm(pcc[:, 0:64], self.cst[:n, C_ONES:C_ONES + 128], gE[:n, 0:64])
            K.act(ecC[:, 0:64], pcc[:, 0:64], AF.Exp)
        def head(h, L, n=n, smp=smp, mI=mI, mLS=mLS, mUS=mUS, nI=nI, nLS=nLS, x0=x0, sm=sm, ecC=ecC):
            qv = x0[:n, h * 128:(h + 1) * 128]
            kv = x0[:n, 512 + h * 128:512 + (h + 1) * 128]
            vv = x0[:n, 1024 + h * 128:1024 + (h + 1) * 128]
            kT, kbT, qT, qeT, dg, DT, MT, M, MTb, Mb, kb, qe, ks, Dl, t1, DTs, STb = self.hd[L * LSTR:L * LSTR + 17]
            K.ts(kb[:n, :], kv, sm[:n, 32 + h:33 + h], ALU.mult)
            K.ts(qe[:n, :], qv, sm[:n, 56 + h:57 + h], ALU.mult)
            K.ts(ks[:n, :], kv, sm[:n, 60 + h:61 + h], ALU.mult)
            headT(self, kT[:, :n], kv, n, 128)
            headT(self, kbT[:, :n], kb[:n, :], n, 128)
            headT(self, qT[:, :n], qv, n, 128)
            headT(self, qeT[:, :n], qe[:n, :], n, 128)
            yield
            K.ts(dg[:n, :n], self.cst[:n, C_ID:C_ID + n], sm[:n, 44 + h:45 + h], ALU.mult)
            pR = mps(self)
            K.mm(pR[:n, :n], self.cst[:n, C_ONES:C_ONES + n], dg[:n, :n])
            K.tt(DT[:n, :n], pR[:n, :n], self.cst[:n, nI:nI + n], ALU.add)
            K.act(DT[:n, :n], DT[:n, :n], AF.Exp, bias=sm[:n, 52 + h:53 + h])
            K.tt(M[:n, :n], pR[:n, :n], self.cst[:n, nLS:nLS + n], ALU.subtract)
            K.act(M[:n, :n], M[:n, :n], AF.Exp, scale=-1.0, bias=sm[:n, 44 + h:45 + h])
            K.tt(DTs[:n, :n], DT[:n, :n], self.cst[:n, mUS:mUS + n], ALU.mult)
            yield
            pg = mps(self)
            K.mm(pg[:n, :n], kT[:, :n], kbT[:, :n])
            K.stt(MT[:n, :n], pg[:n, :n], -1.0, DTs[:n, :n], ALU.mult, ALU.mult)
            pg2 = mps(self)
            K.mm(pg2[:n, :n], kbT[:, :n], kT[:, :n])
            K.stt(M[:n, :n], pg2[:n, :n], -1.0, M[:n, :n], ALU.mult, ALU.mult)
            yield
            if not smp:
                Hs = [Hst[:, h * 128:(h + 1) * 128]]
                kst = [kT[:, :n]]
                qst = [qeT[:, :n]]
                ktl = [ks[:n, :]]
            else:
                K.dma(SH[:, :].re("p (s v) -> p s v", v=128), self.st_gd[l, :, h].re("s k v -> k s v"))
                KTm, QEm, KSm = self.FT[1], self.FT[2], self.FT[5]
                ebv = self.Eb[:, :].re("p (j t) -> p j t", t=128)
                K.tt(KTm[:, :].re("p (j t) -> p j t", t=128), ebv,
                     View(kT, kT.a[:, :].rearrange("p (o t) -> p o t", o=1).to_broadcast([128, 16, 128])), ALU.mult)
                K.tt(QEm[:, :].re("p (j t) -> p j t", t=128), ebv,
                     View(qeT, qeT.a[:, :].rearrange("p (o t) -> p o t", o=1).to_broadcast([128, 16, 128])), ALU.mult)
                kbv = View(ks, ks.a[:, :].rearrange("p (o d) -> p o d", o=1).to_broadcast([128, 16, 128]))
                etv = View(self.cst, self.cst.a[:, C_ET:C_ET + 16].rearrange("p (j o) -> p j o", o=1).to_broadcast([128, 16, 128]))
                K.tt(KSm[:, :].re("p (j d) -> p j d", d=128), kbv, etv, ALU.mult)
                Hs = [SH[:, j * 128:(j + 1) * 128] for j in range(NS)]
                kst = [KTm[:, j * 128:(j + 1) * 128] for j in range(NS)]
                qst = [QEm[:, j * 128:(j + 1) * 128] for j in range(NS)]
                ktl = [KSm[:, j * 128:(j + 1) * 128] for j in range(NS)]
            pk = mps(self)
            for j in range(len(Hs)):
                K.mm(pk[:n, :128], kst[j], Hs[j], start=(j == 0), stop=(j == len(Hs) - 1))
            K.ts(t1[:n, :], vv, sm[:n, 32 + h:33 + h], ALU.mult)
            K.stt(Dl[:n, :], pk[:n, :128], sm[:n, 64 + h:65 + h], t1[:n, :], ALU.mult, ALU.add)
            nsteps = 3 if smp else (7 if n == 128 else 4)
            yield
            yield from solve_unit(self, n, nsteps, MT, M, MTb, Mb, Dl[:n, :], 128)
            jm = {"j": 0}

            def upd(H, psv, jm=jm, h=h, smp=smp):
                col = (h * 16 + jm["j"]) if smp else h
                jm["j"] += 1
                K.stt(H, H, ecC[:, col:col + 1], psv, ALU.mult, ALU.add)
            yield from lin_core(self, n, 128, 128, kT[:, :n], qT[:, :n], DT[:n, :n], qst, Hs, Dl[:n, :], ktl,
                                o_all[:n, h * 128:(h + 1) * 128], upd, ST=STb)
            if smp:
                K.dma(self.o_s["gd"][l, :, h].re("s k v -> k s v"), SH[:, :].re("p (s v) -> p s v", v=128), acc=True)
        run_heads(head, 4, smp)
        rms_heads_gate(self, o_all, n, 4, 128, gate[:n, :], gn, None, r0, 1536)
    K.dma(self.o_p["gd"][l].re("h k v -> k h v"), Hst[:, :].re("p (h v) -> p h v", v=128), acc=True)
    K.dma(self.o_p["conv"][l], self.PROJ[T - 3:T, c0:c0 + 1536], q="sp", acc=True)
    K.dma(self.o_s["conv"][l], self.PROJ[0:128, c0:c0 + 1536].re("(s t) c -> s t c", t=TS)[:, TS - 3:TS, :], q="sp", acc=True)


def mix_rw(self, l):
    K = self.K
    W = self.W
    Hst = self.Hst
    PB = self.WB[0]
    PM = self.WB[1]
    o_all = self.mx[0]
    MU, W0, A0, KKP, KA, RKP, LNG, LNB = 0, 1792, 2304, 2816, 3328, 3840, 4352, 4864
    K.dma(PB[:, MU:MU + 1792], bcast_rows(W["rw_mu"][l], 128))
    for nm, off in [("rw_w0", W0), ("rw_a0", A0), ("rw_kk", KKP), ("rw_ka", KA), ("rw_rk", RKP),
                    ("rw_ln_g", LNG), ("rw_ln_b", LNB)]:
        K.dma(PB[:, off:off + 512], bcast_rows(W[nm][l], 128), acc=True)
    K.dma(PM[:64, 0:512], W["rw_w2"][l])
    K.dma(PM[:64, 512:1024], W["rw_a2"][l], acc=True)
    K.dma(PM[:, 1024:1536], W["rw_g2"][l], acc=True)
    K.memset(Hst.v, 0.0, e="dve")
    K.dma(self.SHS[:, 0:1, :], View(self.st_shift, self.st_shift.a[l].rearrange("s (o c) -> s o c", o=1)), q="sp")
    K.dma(self.SHS[:, 1:9, :], self.PROJ[0:128, 0:RW_COLS].re("(s t) c -> s t c", t=TS), q="sp", acc=True)
    SHb = self.FT[0]
    logw, asig, ga, kk = [self.FT[1][:, i * 512:(i + 1) * 512] for i in range(4)]
    qs, ks, as_, bs = [self.FT[2][:, i * 512:(i + 1) * 512] for i in range(4)]
    kp, ec, tA, tB = [self.FT[3][:, i * 512:(i + 1) * 512] for i in range(4)]
    sm = self.mx[1]
    for ti in range(NT):
        r0, n = self.rows(ti)
        smp = (ti == 0)
        pa, xm = self.xt[0], self.xt[1]
        K.dma(pa[:n, :RW_COLS], self.PROJ[r0:r0 + n, 0:RW_COLS])
        if smp:
            for q in range(NS):
                K.dma(xm[q * TS:(q + 1) * TS, :RW_COLS], self.SHS[q, 0:8, :], acc=(q > 0))
        else:
            K.dma(xm[:n, :RW_COLS], self.PROJ[r0 - 1:r0 - 1 + n, 0:RW_COLS])
            if ti == 1:
                K.memset(xm[0:1, :RW_COLS], 0.0, e="dve")
        K.tt(xm[:n, :RW_COLS], xm[:n, :RW_COLS], pa[:n, :RW_COLS], ALU.subtract)
        K.tt(xm[:n, :RW_COLS], xm[:n, :RW_COLS], PB[:n, MU:MU + RW_COLS], ALU.mult)
        K.tt(xm[:n, :RW_COLS], xm[:n, :RW_COLS], pa[:n, :RW_COLS], ALU.add)
        r_, k_, v_ = xm[:n, 0:512], xm[:n, 512:1024], xm[:n, 1024:1536]
        K.act(pa[:n, 0:64], xm[:n, 1536:1600], AF.Tanh)
        K.act(pa[:n, 128:256], xm[:n, 1664:1792], AF.Sigmoid)
        twT, alT, sgT = self.hd[NLANE * LSTR], self.hd[NLANE * LSTR + 1], self.hd[NLANE * LSTR + 2]
        headT(self, twT[:64, :n], pa[:n, 0:64], n, 64)
        headT(self, alT[:64, :n], xm[:n, 1600:1664], n, 64)
        headT(self, sgT[:, :n], pa[:n, 128:256], n, 128)
        pz = mps(self)
        K.mm(pz[:n, :512], twT[:64, :n], PM[:64, 0:512])
        K.tt(logw[:n], pz[:n, :512], PB[:n, W0:W0 + 512], ALU.add)
        K.act(logw[:n], logw[:n], AF.Sigmoid)
        K.ts(logw[:n], logw[:n], -float(np.exp(-0.5)), ALU.mult)
        pz2 = mps(self)
        K.mm(pz2[:n, :512], alT[:64, :n], PM[:64, 512:1024])
        K.tt(asig[:n], pz2[:n, :512], PB[:n, A0:A0 + 512], ALU.add)
        K.act(asig[:n], asig[:n], AF.Sigmoid)
        pz3 = mps(self)
        K.mm(pz3[:n, :512], sgT[:, :n], PM[:, 1024:1536])
        K.copy(ga[:n], pz3[:n, :512], e="act")
        K.tt(kk[:n], k_, PB[:n, KKP:KKP + 512], ALU.mult)
        K.tt(tA[:n], kk[:n], kk[:n], ALU.mult)
        K.red(sm[:n, 0:8], tA[:n].re("p (h d) -> p h d", d=64))
        K.ts(sm[:n, 0:8], sm[:n, 0:8], 1e-6, ALU.add)
        K.act(sm[:n, 8:16], sm[:n, 0:8], AF.Sqrt)
        K.recip(sm[:n, 16:24], sm[:n, 8:16])
        for h in range(8):
            K.ts(kk[:n, h * 64:(h + 1) * 64], kk[:n, h * 64:(h + 1) * 64], sm[:n, 16 + h:17 + h], ALU.mult)
        K.ts(tA[:n], asig[:n], -1.0, ALU.add)
        K.tt(tA[:n], tA[:n], PB[:n, KA:KA + 512], ALU.mult)
        K.ts(tA[:n], tA[:n], 1.0, ALU.add)
        K.tt(kp[:n], k_, tA[:n], ALU.mult)
        mI = C_MS_I if smp else C_MU_I
        pcs = mps(self)
        K.mm(pcs[:n, :512], self.cst[:n, mI:mI + n], logw[:n])
        K.act(ec[:n], pcs[:n, :512], AF.Exp)
        K.tt(qs[:n], r_, ec[:n], ALU.mult)
        K.tt(tB[:n], pcs[:n, :512], logw[:n], ALU.subtract)
        K.act(tB[:n], tB[:n], AF.Exp)
        K.tt(as_[:n], kk[:n], tB[:n], ALU.mult)
        K.ts(as_[:n], as_[:n], -1.0, ALU.mult)
        K.act(ec[:n], pcs[:n, :512], AF.Exp, scale=-1.0)
        K.tt(ks[:n], kp[:n], ec[:n], ALU.mult)
        K.tt(bs[:n], kk[:n], asig[:n], ALU.mult)
        K.tt(bs[:n], bs[:n], ec[:n], ALU.mult)
        mUS = C_MS_S if smp else C_MU_S
        mLS = C_MSL_S if smp else C_ML_S
        nsteps = 3 if smp else (7 if n == 128 else 4)
        def head(h, L, n=n, smp=smp, mI=mI, mUS=mUS, mLS=mLS, nsteps=nsteps, xm=xm):
            hc = slice(h * 64, (h + 1) * 64)
            aT, bT, kT, qT, MT, M, MTb, Mb, AKT, QBT, QKT, U, sc = self.hd[L * LSTR:L * LSTR + 13]
            headT(self, aT[:64, :n], as_[:n, hc], n, 64)
            headT(self, bT[:64, :n], bs[:n, hc], n, 64)
            yield
            headT(self, kT[:64, :n], ks[:n, hc], n, 64)
            headT(self, qT[:64, :n], qs[:n, hc], n, 64)
            yield
            V = xm[:n, 1024 + h * 64:1024 + (h + 1) * 64]
            pm = mps(self)
            if not smp:
                K.mm(pm[:64, 0:1], logw[:n, hc], self.cst[:n, C_IND + 1:C_IND + 2])
                K.act(sc[:64, 0:1], pm[:64, 0:1], AF.Exp)
                Hs = [Hst[:64, hc]]
                aL, qL, bL, kL = [aT[:64, :n]], [qT[:64, :n]], [bs[:n, hc]], [ks[:n, hc]]
            else:
                K.mm(pm[:64, 0:16], logw[:n, hc], self.cst[:n, C_ET:C_ET + 16])
                K.act(sc[:64, 0:16], pm[:64, 0:16], AF.Exp)
                K.dma(SHb[:64, 0:1024].re("p (s k) -> p s k", k=64), self.st_wkv[l, :, h].re("s v k -> v s k"))
                for half in range(2):
                    pt = mps(self)
                    for q in range(8):
                        j = half * 8 + q
                        K.tr(pt[:64, q * 64:(q + 1) * 64], SHb[:64, j * 64:(j + 1) * 64], self.ident.v)
                    K.copy(SHb[:64, 1024 + half * 512:1024 + (half + 1) * 512], pt[:64, :512])
                Hs = [SHb[:64, 1024 + j * 64:1024 + (j + 1) * 64] for j in range(NS)]
                aTm, qTm = self.FT[4], self.FT[5]
                ebv = self.Eb[:64, :].re("p (j t) -> p j t", t=128)
                K.tt(aTm[:64, :].re("p (j t) -> p j t", t=128), ebv,
                     View(aT, aT.a[:64, :].rearrange("p (o t) -> p o t", o=1).to_broadcast([64, 16, 128])), ALU.mult)
                K.tt(qTm[:64, :].re("p (j t) -> p j t", t=128), ebv,
                     View(qT, qT.a[:64, :].rearrange("p (o t) -> p o t", o=1).to_broadcast([64, 16, 128])), ALU.mult)
                etv = View(self.cst, self.cst.a[:, C_ET:C_ET + 16].rearrange("p (j o) -> p j o", o=1).to_broadcast([128, 16, 64]))
                bsv = bs[:n, hc]
                ksv = ks[:n, hc]
                K.tt(PM[:, 2048:3072].re("p (j d) -> p j d", d=64),
                     View(bsv.buf, bsv.ap.rearrange("p (o d) -> p o d", o=1).to_broadcast([128, 16, 64])), etv, ALU.mult)
                K.tt(PM[:, 3072:4096].re("p (j d) -> p j d", d=64),
                     View(ksv.buf, ksv.ap.rearrange("p (o d) -> p o d", o=1).to_broadcast([128, 16, 64])), etv, ALU.mult)
                aL = [aTm[:64, j * 128:(j + 1) * 128] for j in range(NS)]
                qL = [qTm[:64, j * 128:(j + 1) * 128] for j in range(NS)]
                bL = [PM[:, 2048 + j * 64:2048 + (j + 1) * 64] for j in range(NS)]
                kL = [PM[:, 3072 + j * 64:3072 + (j + 1) * 64] for j in range(NS)]
            p1 = mps(self)
            K.mm(p1[:n, :n], bT[:64, :n], aT[:64, :n])
            K.tt(MT[:n, :n], p1[:n, :n], self.cst[:n, mUS:mUS + n], ALU.mult)
            p2 = mps(self)
            K.mm(p2[:n, :n], aT[:64, :n], bT[:64, :n])
            K.tt(M[:n, :n], p2[:n, :n], self.cst[:n, mLS:mLS + n], ALU.mult)
            yield
            p3 = mps(self)
            K.mm(p3[:n, :n], kT[:64, :n], aT[:64, :n])
            K.tt(AKT[:n, :n], p3[:n, :n], self.cst[:n, mUS:mUS + n], ALU.mult)
            p4 = mps(self)
            K.mm(p4[:n, :n], bT[:64, :n], qT[:64, :n])
            K.tt(QBT[:n, :n], p4[:n, :n], self.cst[:n, mI:mI + n], ALU.mult)
            p5 = mps(self)
            K.mm(p5[:n, :n], kT[:64, :n], qT[:64, :n])
            K.tt(QKT[:n, :n], p5[:n, :n], self.cst[:n, mI:mI + n], ALU.mult)
            yield
            pr = mps(self)
            for j in range(len(Hs)):
                K.mm(pr[:n, :64], aL[j], Hs[j], start=(j == 0), stop=False)
            K.mm(pr[:n, :64], AKT[:n, :n], V, start=False, stop=True)
            K.copy(U[:n, :64], pr[:n, :64], e="dve")
            yield
            yield from solve_unit(self, n, nsteps, MT, M, MTb, Mb, U[:n, :64], 64)
            po = mps(self)
            for j in range(len(Hs)):
                K.mm(po[:n, :64], qL[j], Hs[j], start=(j == 0), stop=False)
            K.mm(po[:n, :64], QBT[:n, :n], U[:n, :64], start=False, stop=False)
            K.mm(po[:n, :64], QKT[:n, :n], V, start=False, stop=True)
            K.copy(o_all[:n, hc], po[:n, :64], e="act")
            yield
            for j0 in range(0, len(Hs), 8):
                pu = mps(self)
                js = list(range(j0, min(j0 + 8, len(Hs))))
                for q, j in enumerate(js):
                    K.mm(pu[:64, q * 64:(q + 1) * 64], bL[j], U[:n, :64], start=True, stop=False)
                    K.mm(pu[:64, q * 64:(q + 1) * 64], kL[j], V, start=False, stop=True)
                for q, j in enumerate(js):
                    K.tt(Hs[j], Hs[j], pu[:64, q * 64:(q + 1) * 64], ALU.add)
                    K.ts(Hs[j], Hs[j], sc[:64, j:j + 1], ALU.mult)
            if smp:
                for half in range(2):
                    pt = mps(self)
                    for q in range(8):
                        j = half * 8 + q
                        K.tr(pt[:64, q * 64:(q + 1) * 64], Hs[j], self.ident.v)
                    K.copy(SHb[:64, half * 512:(half + 1) * 512], pt[:64, :512])
                K.dma(self.o_s["wkv"][l, :, h].re("s v k -> v s k"), SHb[:64, 0:1024].re("p (s k) -> p s k", k=64), acc=True)
        run_heads(head, 8, smp)
        K.red(sm[:n, 0:8], o_all[:n, :].re("p (h d) -> p h d", d=64))
        K.tt(tA[:n], o_all[:n, :], o_all[:n, :], ALU.mult)
        K.red(sm[:n, 8:16], tA[:n].re("p (h d) -> p h d", d=64))
        K.ts(sm[:n, 0:8], sm[:n, 0:8], 1.0 / 64, ALU.mult)
        K.tt(sm[:n, 16:24], sm[:n, 0:8], sm[:n, 0:8], ALU.mult)
        K.ts(sm[:n, 8:16], sm[:n, 8:16], 1.0 / 64, ALU.mult)
        K.tt(sm[:n, 8:16], sm[:n, 8:16], sm[:n, 16:24], ALU.subtract)
        K.ts(sm[:n, 8:16], sm[:n, 8:16], 64e-5, ALU.add)
        K.act(sm[:n, 16:24], sm[:n, 8:16], AF.Sqrt)
        K.recip(sm[:n, 24:32], sm[:n, 16:24])
        for h in range(8):
            hc = slice(h * 64, (h + 1) * 64)
            K.ts(o_all[:n, hc], o_all[:n, hc], sm[:n, h:h + 1], ALU.subtract, sm[:n, 24 + h:25 + h], ALU.mult)
        K.tt(o_all[:n, :], o_all[:n, :], PB[:n, LNG:LNG + 512], ALU.mult)
        K.tt(o_all[:n, :], o_all[:n, :], PB[:n, LNB:LNB + 512], ALU.add)
        K.tt(tA[:n], r_, kp[:n], ALU.mult)
        K.tt(tA[:n], tA[:n], PB[:n, RKP:RKP + 512], ALU.mult)
        K.red(sm[:n, 32:40], tA[:n].re("p (h d) -> p h d", d=64))
        for h in range(8):
            hc = slice(h * 64, (h + 1) * 64)
            K.stt(o_all[:n, hc], xm[:n, 1024 + h * 64:1024 + (h + 1) * 64], sm[:n, 32 + h:33 + h], o_all[:n, hc], ALU.mult, ALU.add)
        K.tt(o_all[:n, :], o_all[:n, :], ga[:n], ALU.mult)
        K.dma(self.OB[r0:r0 + n, 0:512], o_all[:n, :512], acc=True)
    for half in range(1):
        pt = mps(self)
        for h in range(8):
            K.tr(pt[:64, h * 64:(h + 1) * 64], Hst[:64, h * 64:(h + 1) * 64], self.ident.v)
        K.copy(SHb[:64, 0:512], pt[:64, :512])
    K.dma(self.o_p["wkv"][l].re("h v k -> v h k"), SHb[:64, 0:512].re("p (h k) -> p h k", k=64), acc=True)
    K.dma(self.o_p["shift"][l:l + 1, :], self.PROJ[T - 1:T, 0:RW_COLS], q="sp", acc=True)
    K.dma(self.o_s["shift"][l], self.PROJ[0:128, 0:RW_COLS].re("(s t) c -> s t c", t=TS)[:, TS - 1, :], q="sp", acc=True)


MIXERS = {"ret": mix_ret, "hg": mix_hg, "gd": mix_gd, "rw": mix_rw}
```
